# Optimizing a Trainium2 kernel written in Bass

```python
import math
import jax, jax.numpy as jnp
from jax import lax
import numpy as np

D_MODEL = 1024
BATCH = 16
SEQ = 2048
DEPTH = 4

N_EVEN = (DEPTH + 1) // 2
N_ODD = DEPTH // 2
D_FF = 4 * D_MODEL
NORM_EPS = 1e-5

SSD_WIDTH = D_MODEL
SSD_HEAD_DIM = 64
SSD_HEADS = SSD_WIDTH // SSD_HEAD_DIM
SSD_GROUPS = 2
SSD_STATE = 128
SSD_CONV = 4
SSD_CHUNK = 128
SSD_XBC = SSD_WIDTH + 2 * SSD_GROUPS * SSD_STATE
SSD_IN = SSD_WIDTH + SSD_XBC + SSD_HEADS

GMLP_WIDTH = D_MODEL
GMLP_GROUPS = 8
GMLP_GROUP_DIM = GMLP_WIDTH // GMLP_GROUPS
GMLP_CHUNK = 128
EVEN_IN = SSD_IN + 2 * GMLP_WIDTH
EVEN_MIX = SSD_WIDTH + GMLP_WIDTH

CONF_WIDTH = D_MODEL
CONF_KERNEL = 31

HGRN_HEADS = 8
HGRN_EXPAND = 128
HGRN_HEAD_V = D_MODEL // HGRN_HEADS
HGRN_K = HGRN_HEADS * HGRN_EXPAND
HGRN_V = HGRN_HEADS * HGRN_HEAD_V
HGRN_CHUNK = 64
ODD_IN = 2 * CONF_WIDTH + 2 * HGRN_K + 2 * HGRN_V
ODD_MIX = CONF_WIDTH + HGRN_V

kernel_name = "hybrid_ssd_gmlp_conformer_hgrn2_trunk"


def rmsnorm(x, g):
    xf = x.astype(jnp.float32)
    y = xf * lax.rsqrt(jnp.mean(xf * xf, axis=-1, keepdims=True) + NORM_EPS)
    return (y * g).astype(x.dtype)


def layernorm(x, g, b):
    xf = x.astype(jnp.float32)
    mu = jnp.mean(xf, axis=-1, keepdims=True)
    var = jnp.mean(jnp.square(xf - mu), axis=-1, keepdims=True)
    return ((xf - mu) * lax.rsqrt(var + NORM_EPS) * g + b).astype(x.dtype)


def causal_depthwise_conv(x, w, b):
    width = w.shape[0]
    xp = jnp.pad(x, ((0, 0), (width - 1, 0), (0, 0)))
    y = lax.conv_general_dilated(xp, w[:, None, :].astype(x.dtype), window_strides=(1,), padding="VALID",
                                 dimension_numbers=("NWC", "WIO", "NWC"), feature_group_count=x.shape[-1])
    return y + b


def ssd_mixer(proj, conv_w, conv_b, dt_bias, a_log, d_skip, norm_g):
    bsz, seqlen, _ = proj.shape
    nc, q = seqlen // SSD_CHUNK, SSD_CHUNK
    g, hpg, p, n = SSD_GROUPS, SSD_HEADS // SSD_GROUPS, SSD_HEAD_DIM, SSD_STATE
    z, xbc, dt = jnp.split(proj, [SSD_WIDTH, SSD_WIDTH + SSD_XBC], axis=-1)
    xbc = jax.nn.silu(causal_depthwise_conv(xbc, conv_w, conv_b))
    xs, bm, cm = jnp.split(xbc, [SSD_WIDTH, SSD_WIDTH + g * n], axis=-1)
    xs = xs.astype(jnp.float32).reshape(bsz, nc, q, g, hpg, p)
    bm = bm.astype(jnp.float32).reshape(bsz, nc, q, g, n)
    cm = cm.astype(jnp.float32).reshape(bsz, nc, q, g, n)
    dt = jax.nn.softplus(dt.astype(jnp.float32).reshape(bsz, nc, q, g, hpg) + dt_bias.reshape(g, hpg))
    a = -jnp.exp(a_log.astype(jnp.float32)).reshape(g, hpg)
    acs = jnp.cumsum(dt * a, axis=2)
    xdt = xs * dt[..., None]
    causal = jnp.tril(jnp.ones((q, q), dtype=bool))
    seg = acs[:, :, :, None] - acs[:, :, None, :]
    lmat = jnp.exp(jnp.where(causal[:, :, None, None], seg, -jnp.inf))
    cb = jnp.einsum("bclgn,bcsgn->bclsg", cm, bm)
    y_diag = jnp.einsum("bclsgj,bcsgjp->bclgjp", cb[..., None] * lmat, xdt)
    decay_to_end = jnp.exp(acs[:, :, -1:] - acs)
    decay_from_start = jnp.exp(acs)
    chunk_decay = jnp.exp(acs[:, :, -1])

    def step(h, inp):
        b_c, c_c, x_c, dte, dfs, cd = inp
        y_off = jnp.einsum("blgn,bgjpn,blgj->blgjp", c_c, h, dfs)
        h = h * cd[..., None, None] + jnp.einsum("blgn,blgj,blgjp->bgjpn", b_c, dte, x_c)
        return h, y_off

    h0 = jnp.zeros((bsz, g, hpg, p, n), jnp.float32)
    sw = lambda t: jnp.moveaxis(t, 1, 0)
    _, y_off = lax.scan(step, h0, (sw(bm), sw(cm), sw(xdt), sw(decay_to_end), sw(decay_from_start), sw(chunk_decay)))
    y = y_diag + jnp.moveaxis(y_off, 0, 1) + d_skip.astype(jnp.float32).reshape(g, hpg)[:, :, None] * xs
    y = y.reshape(bsz, seqlen, SSD_WIDTH) * jax.nn.silu(z.astype(jnp.float32))
    yg = y.reshape(bsz, seqlen, g, SSD_WIDTH // g)
    yg = yg * lax.rsqrt(jnp.mean(yg * yg, axis=-1, keepdims=True) + NORM_EPS)
    return (yg.reshape(bsz, seqlen, SSD_WIDTH) * norm_g).astype(proj.dtype)


def gmlp_mixer(proj, ln_g, ln_b, w_s, b_s):
    bsz, seqlen, _ = proj.shape
    nc, q = seqlen // GMLP_CHUNK, GMLP_CHUNK
    u, v = jnp.split(jax.nn.gelu(proj, approximate=False), 2, axis=-1)
    v = layernorm(v, ln_g, ln_b).reshape(bsz, nc, q, GMLP_GROUPS, GMLP_GROUP_DIM)
    causal = jnp.tril(jnp.ones((q, q), dtype=bool))
    w = jnp.where(causal[None], w_s, 0.0)
    mixed = jnp.einsum("gts,bcsgd->bctgd", w, v) + jnp.transpose(b_s)[:, :, None]
    return (u * mixed.reshape(bsz, seqlen, GMLP_WIDTH)).astype(proj.dtype)


def conformer_conv_mixer(proj, conv_w, conv_b, ln_g, ln_b):
    a, gate = jnp.split(proj, 2, axis=-1)
    h = a * jax.nn.sigmoid(gate)
    h = causal_depthwise_conv(h, conv_w, conv_b)
    return jax.nn.silu(layernorm(h, ln_g, ln_b)).astype(proj.dtype)


def hgrn2_mixer(proj, lower_bound, norm_g):
    bsz, seqlen, _ = proj.shape
    nc, c, h, dk, dv = seqlen // HGRN_CHUNK, HGRN_CHUNK, HGRN_HEADS, HGRN_EXPAND, HGRN_HEAD_V
    qx, fx, ix, gx = jnp.split(proj, [HGRN_K, 2 * HGRN_K, 2 * HGRN_K + HGRN_V], axis=-1)
    fx = fx.astype(jnp.float32).reshape(bsz, nc, c, h, dk)
    lb = lower_bound.reshape(h, dk)
    log_f = jnp.logaddexp(jnp.log(lb), jnp.log1p(-lb) + jax.nn.log_sigmoid(fx))
    k = (1.0 - lb) * jax.nn.sigmoid(-fx)
    qv = qx.astype(jnp.float32).reshape(bsz, nc, c, h, dk) * (dk ** -0.5)
    iv = ix.astype(jnp.float32).reshape(bsz, nc, c, h, dv)
    bcum = jnp.cumsum(log_f, axis=2)
    mid = bcum[:, :, c // 2:c // 2 + 1]
    qg = qv * jnp.exp(bcum - mid)
    kg = k * jnp.exp(mid - bcum)
    causal = jnp.tril(jnp.ones((c, c), dtype=bool))
    att = jnp.where(causal, jnp.einsum("bnthk,bnshk->bnhts", qg, kg), 0.0)
    o_intra = jnp.einsum("bnhts,bnshv->bnthv", att, iv)
    q_dec = qv * jnp.exp(bcum)
    k_dec = k * jnp.exp(bcum[:, :, -1:] - bcum)
    chunk_decay = jnp.exp(bcum[:, :, -1])

    def step(s, inp):
        qd, kd, vc, cd = inp
        o = jnp.einsum("blhk,bhkv->blhv", qd, s)
        s = s * cd[..., None] + jnp.einsum("blhk,blhv->bhkv", kd, vc)
        return s, o

    s0 = jnp.zeros((bsz, h, dk, dv), jnp.float32)
    sw = lambda t: jnp.moveaxis(t, 1, 0)
    _, o_inter = lax.scan(step, s0, (sw(q_dec), sw(k_dec), sw(iv), sw(chunk_decay)))
    o = (o_intra + jnp.moveaxis(o_inter, 0, 1)).reshape(bsz, seqlen, h, dv)
    o = o * lax.rsqrt(jnp.mean(o * o, axis=-1, keepdims=True) + NORM_EPS) * norm_g.reshape(h, dv)
    o = o.reshape(bsz, seqlen, HGRN_V) * jax.nn.silu(gx.astype(jnp.float32))
    return o.astype(proj.dtype)


def setup_inputs(seed: int = 0) -> dict:
    key = jax.random.key(seed)
    ks = iter(jax.random.split(key, 32))
    nrm = lambda shape, scale: jax.random.normal(next(ks), shape, jnp.float32) * scale
    gain = lambda shape: 1.0 + nrm(shape, 0.02)
    dt0 = jnp.exp(jax.random.uniform(next(ks), (N_EVEN, SSD_HEADS), jnp.float32, math.log(1e-3), math.log(1e-1)))
    return {
        "x": nrm((BATCH, SEQ, D_MODEL), 1.0),
        "even_w_in": nrm((N_EVEN, D_MODEL, EVEN_IN), D_MODEL ** -0.5),
        "even_w_out": nrm((N_EVEN, EVEN_MIX, D_MODEL), EVEN_MIX ** -0.5),
        "ssd_conv_w": nrm((N_EVEN, SSD_CONV, SSD_XBC), SSD_CONV ** -0.5),
        "ssd_conv_b": nrm((N_EVEN, SSD_XBC), 0.02),
        "ssd_dt_bias": dt0 + jnp.log(-jnp.expm1(-dt0)),
        "ssd_a_log": jnp.log(jax.random.uniform(next(ks), (N_EVEN, SSD_HEADS), jnp.float32, 1.0, 16.0)),
        "ssd_d": gain((N_EVEN, SSD_HEADS)),
        "ssd_norm_g": gain((N_EVEN, SSD_WIDTH)),
        "gmlp_ln_g": gain((N_EVEN, GMLP_WIDTH)),
        "gmlp_ln_b": nrm((N_EVEN, GMLP_WIDTH), 0.02),
        "gmlp_w_s": nrm((N_EVEN, GMLP_GROUPS, GMLP_CHUNK, GMLP_CHUNK), GMLP_CHUNK ** -0.5),
        "gmlp_b_s": gain((N_EVEN, GMLP_GROUPS, GMLP_CHUNK)),
        "odd_w_in": nrm((N_ODD, D_MODEL, ODD_IN), D_MODEL ** -0.5),
        "odd_w_out": nrm((N_ODD, ODD_MIX, D_MODEL), ODD_MIX ** -0.5),
        "conf_conv_w": nrm((N_ODD, CONF_KERNEL, CONF_WIDTH), CONF_KERNEL ** -0.5),
        "conf_conv_b": nrm((N_ODD, CONF_WIDTH), 0.02),
        "conf_ln_g": gain((N_ODD, CONF_WIDTH)),
        "conf_ln_b": nrm((N_ODD, CONF_WIDTH), 0.02),
        "hgrn_lb_logits": nrm((N_ODD, HGRN_K), 0.1),
        "hgrn_norm_g": gain((N_ODD, HGRN_V)),
        "mix_norm_g": gain((DEPTH, D_MODEL)),
        "ffn_norm_g": gain((DEPTH, D_MODEL)),
        "ffn_w1": nrm((DEPTH, D_MODEL, D_FF), D_MODEL ** -0.5),
        "ffn_w2": nrm((DEPTH, D_FF, D_MODEL), D_FF ** -0.5),
        "final_norm_g": gain((D_MODEL,)),
    }


def reference(x, even_w_in, even_w_out, ssd_conv_w, ssd_conv_b, ssd_dt_bias, ssd_a_log, ssd_d, ssd_norm_g,
              gmlp_ln_g, gmlp_ln_b, gmlp_w_s, gmlp_b_s, odd_w_in, odd_w_out, conf_conv_w, conf_conv_b,
              conf_ln_g, conf_ln_b, hgrn_lb_logits, hgrn_norm_g, mix_norm_g, ffn_norm_g, ffn_w1, ffn_w2,
              final_norm_g):
    lb_all = jnp.cumsum(jax.nn.softmax(hgrn_lb_logits.astype(jnp.float32), axis=0), axis=0)
    lb_all = lb_all - lb_all[:1]
    for layer in range(DEPTH):
        h = rmsnorm(x, mix_norm_g[layer])
        if layer % 2 == 0:
            e = layer // 2
            proj = h @ even_w_in[e]
            ssd_y = ssd_mixer(proj[..., :SSD_IN], ssd_conv_w[e], ssd_conv_b[e], ssd_dt_bias[e],
                              ssd_a_log[e], ssd_d[e], ssd_norm_g[e])
            gmlp_y = gmlp_mixer(proj[..., SSD_IN:], gmlp_ln_g[e], gmlp_ln_b[e], gmlp_w_s[e], gmlp_b_s[e])
            y = jnp.concatenate([ssd_y, gmlp_y], axis=-1) @ even_w_out[e]
        else:
            o = layer // 2
            proj = h @ odd_w_in[o]
            conf_y = conformer_conv_mixer(proj[..., :2 * CONF_WIDTH], conf_conv_w[o], conf_conv_b[o],
                                          conf_ln_g[o], conf_ln_b[o])
            hgrn_y = hgrn2_mixer(proj[..., 2 * CONF_WIDTH:], lb_all[o], hgrn_norm_g[o])
            y = jnp.concatenate([conf_y, hgrn_y], axis=-1) @ odd_w_out[o]
        x = x + y
        h = rmsnorm(x, ffn_norm_g[layer])
        x = x + jnp.square(jax.nn.relu(h @ ffn_w1[layer])) @ ffn_w2[layer]
    return rmsnorm(x, final_norm_g)
```

```python
import numpy as np
from contextlib import ExitStack
import concourse.bass as bass
import concourse.mybir as mybir
import concourse.bass_utils as bu

F32 = mybir.dt.float32
BF16 = mybir.dt.bfloat16
AF = mybir.ActivationFunctionType
ALU = mybir.AluOpType
AX = mybir.AxisListType

D = 1024
DFF = 4096
EPS = 1e-5
ENGS = ["pe", "act", "dve", "pool", "sp"]


class Op:
    __slots__ = ("eng", "fn", "deps", "odeps", "dma", "dma_val", "needs_inc", "inc_idx", "cost", "lat", "keep", "dkey", "grp")


class _Probe:
    def __getattr__(self, name):
        def f(*a, **kw):
            self.rec = (name, a, kw)
            return None
        return f


def _act_group(fn):
    pr = _Probe()
    pr.rec = None
    try:
        fn(pr)
    except Exception:
        return None
    if pr.rec is None:
        return None
    name, a, kw = pr.rec
    if name != "activation":
        return None
    f = kw.get("func", None)
    if f in (AF.Copy, AF.Identity, AF.Relu):
        return None
    if f in (AF.Exp, AF.Ln):
        return "explog"
    return str(f)


def _op_cost(eng, fn, dma):
    pr = _Probe()
    pr.rec = None
    try:
        fn(pr)
    except Exception:
        pr.rec = None
    if pr.rec is None:
        return 500.0, 500.0
    name, a, kw = pr.rec
    out = kw.get("out", a[0] if a else None)
    try:
        shp = list(out.shape)
        n = 1
        for d in shp[1:]:
            n *= int(d)
        npart = int(shp[0])
    except Exception:
        n, npart = 512, 128
    if dma is not None:
        nbytes = n * npart * (2 if out.dtype == BF16 else 4)
        issue = 1500.0 if eng == "pool" else 120.0
        return issue, 2500.0 + nbytes / 150.0
    if eng == "pe":
        c = max(n, 64) / 2.4 + 12.0
        lhsT = kw.get("lhsT", None)
        if lhsT is not None and lhsT.dtype == F32:
            c *= 4.0
        return c, c + 160.0
    if eng == "act":
        c = 200.0 + n * 0.75
    elif eng == "dve":
        c = 90.0 + n * (1.05 if out.dtype == F32 or name in ("tensor_tensor_scan",) else 0.8)
    else:
        c = 250.0 + n * 2.0
    return c, c


class Prog:
    def __init__(self):
        self.ops = []
        self.lastw = {}
        self.rds = {}
        self.dma_cnt = {}
        self.dma_slot = {}
        self.free_slots_e = {}
        self.slot_eng = {}
        self.nslots = 0

    def _slot(self, key, eng):
        if key not in self.dma_slot:
            fl = self.free_slots_e.setdefault(eng, [])
            if fl:
                sl = fl.pop()
            else:
                sl = self.nslots
                self.nslots += 1
                self.slot_eng[sl] = eng
            self.dma_slot[key] = sl
        return self.dma_slot[key]

    def add(self, eng, fn, r=(), w=(), dma=None):
        op = Op()
        op.eng = eng
        op.fn = fn
        op.dkey = dma
        if dma is not None:
            dma = self._slot(dma, eng)
        op.dma = dma
        op.needs_inc = False
        op.inc_idx = 0
        op.dma_val = 0
        deps = {}
        for k in r:
            lw = self.lastw.get(k)
            if lw is not None:
                deps[lw] = "raw"
        for k in w:
            lw = self.lastw.get(k)
            if lw is not None and lw not in deps:
                deps[lw] = "waw"
            for rd in self.rds.get(k, ()):
                if rd not in deps:
                    deps[rd] = "war"
        i = len(self.ops)
        fd = []
        for a, typ in deps.items():
            A = self.ops[a]
            if A.dma is None:
                if A.eng == eng and typ != "raw" and dma is None:
                    continue
                A.needs_inc = True
            fd.append(a)
        op.deps = fd
        op.odeps = list(deps.keys())
        op.cost, op.lat = _op_cost(eng, fn, dma)
        op.grp = _act_group(fn) if (eng == "act" and dma is None) else None
        if dma is not None:
            self.dma_cnt[dma] = self.dma_cnt.get(dma, 0) + 16
            op.dma_val = self.dma_cnt[dma]
        self.ops.append(op)
        for k in r:
            self.rds.setdefault(k, []).append(i)
        for k in w:
            self.lastw[k] = i
            self.rds[k] = []
        return i

    def barrier(self, keep=()):
        keep = set(keep)
        for eng in ENGS:
            op = Op()
            op.eng = eng
            op.fn = None
            op.dma = None
            op.needs_inc = False
            op.inc_idx = 0
            op.dma_val = 0
            op.deps = []
            op.odeps = []
            op.cost = op.lat = 0.0
            op.keep = keep
            op.dkey = None
            op.grp = None
            self.ops.append(op)
        iskept = lambda k: isinstance(k, tuple) and k[0] in keep
        for k in list(self.dma_slot):
            if not iskept(k):
                sl = self.dma_slot.pop(k)
                self.free_slots_e.setdefault(self.slot_eng[sl], []).append(sl)
        self.lastw = {k: v for k, v in self.lastw.items() if iskept(k)}
        self.rds = {k: [] for k in self.lastw}

    def _fix_barriers(self):
        last = {}
        for i, op in enumerate(self.ops):
            if op.fn is None:
                op.deps = [a for k, a in last.items() if not (k[0] == "e" and k[1] == op.eng)
                           and not (k[0] == "d" and isinstance(self.ops[a].dkey, tuple) and self.ops[a].dkey[0] in op.keep)]
                for a in op.deps:
                    if self.ops[a].dma is None:
                        self.ops[a].needs_inc = True
                continue
            if op.dma is not None:
                k = ("d", op.dma)
                if k not in last or self.ops[last[k]].dma_val < op.dma_val:
                    last[k] = i
            else:
                last[("e", op.eng)] = i

    def schedule(self, xlat=350.0):
        import heapq
        ops = self.ops
        n = len(ops)
        new_order = []
        i = 0
        while i < n:
            if ops[i].fn is None:
                new_order.append(i)
                i += 1
                continue
            j = i
            while j < n and ops[j].fn is not None:
                j += 1
            seg = range(i, j)
            indeg = {}
            users = {}
            for k in seg:
                cnt = 0
                for a in ops[k].odeps:
                    if a >= i:
                        cnt += 1
                        users.setdefault(a, []).append(k)
                indeg[k] = cnt
            blev = {}
            for k in reversed(seg):
                m_ = 0.0
                for u in users.get(k, ()):
                    v = blev[u] + (xlat if ops[u].eng != ops[k].eng else 0.0)
                    if v > m_:
                        m_ = v
                blev[k] = ops[k].lat + m_
            fin = {}
            last_grp = None
            efree = {e: 0.0 for e in ENGS}
            ready = {e: [] for e in ENGS}
            rtime = {}
            for k in seg:
                if indeg[k] == 0:
                    rtime[k] = 0.0
                    heapq.heappush(ready[ops[k].eng], (0.0, k))
            done = 0
            total = j - i
            while done < total:
                best = None
                for e in ENGS:
                    h = ready[e]
                    if not h:
                        continue
                    ef = efree[e]
                    rt, k = h[0]
                    if rt <= ef:
                        cand = []
                        slack_ = 500.0 if (e == "act" and ACT_TABLE_AWARE) else 0.0
                        while h and h[0][0] <= ef + slack_:
                            cand.append(heapq.heappop(h))
                        if e == "act" and ACT_TABLE_AWARE:
                            lg_ = last_grp
                            kk = max(cand, key=lambda c: (1 if (ops[c[1]].grp is None or ops[c[1]].grp == lg_) else 0,
                                                          blev[c[1]], -c[1]))[1]
                        else:
                            kk = max(cand, key=lambda c: (blev[c[1]], -c[1]))[1] if PRIO_CP else min(c[1] for c in cand)
                        for c in cand:
                            if c[1] != kk:
                                heapq.heappush(h, (c[0], c[1]))
                        st_ = max(ef, rtime[kk])
                        k = kk
                        popped = True
                    else:
                        st_ = rt
                        popped = False
                    if best is None or st_ < best[0] or (st_ == best[0] and k < best[1]):
                        if best is not None and best[3]:
                            heapq.heappush(ready[ops[best[1]].eng], (rtime[best[1]], best[1]))
                        best = (st_, k, e, popped)
                    elif popped:
                        heapq.heappush(h, (rtime[k], k))
                st_, k, e, popped = best
                if not popped:
                    heapq.heappop(ready[e])
                op = ops[k]
                sw_ = 0.0
                if e == "act" and op.grp is not None:
                    if op.grp != last_grp:
                        sw_ = 1300.0
                    last_grp = op.grp
                efree[e] = st_ + op.cost + sw_
                fin[k] = st_ + op.lat + sw_
                new_order.append(k)
                done += 1
                for u in users.get(k, ()):
                    indeg[u] -= 1
                    t_ = fin[k] + (xlat if (ops[u].eng != e or op.dma is not None) else 120.0)
                    if t_ > rtime.get(u, 0.0):
                        rtime[u] = t_
                    if indeg[u] == 0:
                        heapq.heappush(ready[ops[u].eng], (rtime[u], u))
            i = j
        remap = {old: new for new, old in enumerate(new_order)}
        nops = [ops[k] for k in new_order]
        for op in nops:
            op.deps = [remap[a] for a in op.deps]
            op.odeps = [remap[a] for a in op.odeps]
        self.ops = nops

    def emit(self, nc):
        self._fix_barriers()
        cnt = {e: 0 for e in ENGS}
        by_eng = {e: [] for e in ENGS}
        for op in self.ops:
            if op.dma is None and op.needs_inc:
                cnt[op.eng] += 1
                op.inc_idx = cnt[op.eng]
            by_eng[op.eng].append(op)
        with ExitStack() as es:
            esem = {e: es.enter_context(nc.semaphore("s_" + e)) for e in ENGS}
            dsem = {}
            for n, k in enumerate(self.dma_cnt):
                dsem[k] = es.enter_context(nc.semaphore("d%d" % n))
            block = es.enter_context(nc.Block())
            ops = self.ops
            dma_cnt = self.dma_cnt

            def body(e, eng):
                waited = {}
                for op in by_eng[eng]:
                    need = {}
                    for a in op.deps:
                        A = ops[a]
                        if A.dma is not None:
                            s, v = dsem[A.dma], A.dma_val
                        else:
                            s, v = esem[A.eng], A.inc_idx
                        if need.get(s, (None, 0))[1] < v:
                            need[s] = (s, v)
                    for s, v in need.values():
                        if waited.get(s, 0) < v:
                            e.wait_ge(s, v)
                            waited[s] = v
                    if op.fn is None:
                        continue
                    ins = op.fn(e)
                    if op.dma is not None:
                        ins.then_inc(dsem[op.dma], 16)
                    elif op.needs_inc:
                        ins.then_inc(esem[eng], 1)
                if eng == "sp":
                    for k, s in dsem.items():
                        if waited.get(s, 0) < dma_cnt[k]:
                            e.wait_ge(s, dma_cnt[k])

            block.tensor(lambda e: body(e, "pe"))
            block.scalar(lambda e: body(e, "act"))
            block.vector(lambda e: body(e, "dve"))
            block.gpsimd(lambda e: body(e, "pool"))
            block.sync(lambda e: body(e, "sp"))


class Arena:
    def __init__(self, ap, nwords):
        self.ap = ap
        self.n = nwords
        self.off = 0

    def mark(self):
        return self.off

    def reset(self, m):
        self.off = m

    def f32(self, n):
        a = self.off
        self.off += (n + 7) // 8 * 8
        assert self.off <= self.n, ("SBUF arena overflow", self.off, self.n)
        return self.ap[:, a:a + n]

    def bf16(self, n):
        w = (n + 1) // 2
        a = self.off
        self.off += (w + 7) // 8 * 8
        assert self.off <= self.n, ("SBUF arena overflow", self.off, self.n)
        return self.ap[:, a:a + w].bitcast(BF16)[:, 0:n]


ARENA_WORDS = 53200


class Builder:
    def __init__(self, ntok, seq, layers, nrm_final=True):
        self.ntok = ntok
        self.seq = seq
        self.layers = layers
        self.nrm_final = nrm_final
        self.nc = bass.Bass("TRN2", target_bir_lowering=False)
        self.P = Prog()
        self.dram = {}

    def din(self, name, shape):
        t = self.nc.dram_tensor(name, list(shape), F32, kind="ExternalInput").ap()
        self.dram[name] = t
        return t

    def setup(self, es):
        nc = self.nc
        arena_t = es.enter_context(nc.sbuf_tensor("arena", [128, ARENA_WORDS], F32))
        self.A = Arena(arena_t, ARENA_WORDS)
        self.psum = es.enter_context(nc.psum_tensor("psum", [128, 8, 512], F32))
        A, P = self.A, self.P
        self.idf = A.f32(128)
        self.idb = A.bf16(128)
        ident = self.dram["c_ident"]
        P.add("sp", lambda e: e.dma_start(out=self.idf, in_=ident), w=["idf"], dma="c_idf")
        P.add("dve", lambda e: e.tensor_copy(out=self.idb, in_=self.idf), r=["idf"], w=["idb"])
        self.onesq = A.bf16(128)
        P.add("dve", lambda e: e.memset(self.onesq, 1.0), w=["onesq"])
        self.base_mark = A.mark()

    def bank(self, i):
        return self.psum[:, i, :]

    def dump(self, name, ap, keys):
        if not getattr(self, "debug", False):
            return
        if name in self.dram:
            return
        shp = list(ap.shape)
        t = self.nc.dram_tensor("dbg_" + name, shp, ap.dtype, kind="ExternalOutput").ap()
        self.dram[name] = t
        self.P.add("sp", lambda e: e.dma_start(out=t, in_=ap), r=list(keys), w=[("dbg", name)], dma=("dbg", name))

    def bank_bf(self, i):
        return self.psum[:, i, :].bitcast(BF16)

    def load_bcast(self, dst, src_row, key):
        self.P.add("sp", lambda e: e.dma_start(out=dst, in_=src_row.partition_broadcast(128)), w=[key], dma=key)

    def load_w(self, dst, src, key, nsplit):
        n = dst.shape[2]
        nsplit = max(1, min(nsplit, n // 128))
        step = n // nsplit
        if not hasattr(self, "wsplit"):
            self.wsplit = {}
        self.wsplit[key] = (n, step, (dst.shape[1] + 7) // 8)
        K = dst.shape[1]
        for j in range(nsplit):
            c0, c1 = j * step, (n if j == nsplit - 1 else (j + 1) * step)
            for k0 in range(0, K, 8):
                k1 = min(K, k0 + 8)
                self.P.add("pool", lambda e, c0=c0, c1=c1, k0=k0, k1=k1: e.dma_start(out=dst[:, k0:k1, c0:c1],
                                                                                   in_=src[:, k0:k1, c0:c1]),
                           w=[(key, j, k0 // 8)], dma=(key, j, k0 // 8))

    def wk(self, key, c0, n):
        tot, step, nk = self.wsplit[key]
        j0 = min(c0 // step, (tot - 1) // step)
        j1 = min((c0 + n - 1) // step, (tot - 1) // step)
        nmax = (tot + step - 1) // step - 1
        return [(key, min(j, nmax), kk) for j in range(j0, j1 + 1) for kk in range(nk)]

    def norm_tile(self, xt, xkey, gt, gkey, hb, hbkey, scr):
        P = self.P
        junk, ss, rs, sk = scr["junk"], scr["ss"], scr["rs"], scr["key"]
        jkey = scr.get("jkey", sk + "j")
        P.add("act", lambda e: e.activation(out=junk, in_=xt, func=AF.Square, accum_out=ss), r=[xkey], w=[jkey, sk + "ss"])
        P.add("dve", lambda e: e.tensor_scalar(out=ss, in0=ss, scalar1=1.0 / D, scalar2=EPS, op0=ALU.mult, op1=ALU.add),
              r=[sk + "ss"], w=[sk + "ss"])
        P.add("act", lambda e: e.activation(out=rs, in_=ss, func=AF.Sqrt), r=[sk + "ss"], w=[sk + "rs"])
        P.add("dve", lambda e: e.reciprocal(out=rs, in_=rs), r=[sk + "rs"], w=[sk + "rs"])
        P.add("dve", lambda e: e.scalar_tensor_tensor(out=hb, in0=xt, scalar=rs, in1=gt, op0=ALU.mult, op1=ALU.mult),
              r=[xkey, sk + "rs", gkey], w=[hbkey])

    def transpose_tile(self, hb, hbkey, hT, hTkey, col0, pbank, nchunk=8):
        P = self.P
        pT = self.bank_bf(pbank)
        pk = ("ps", pbank)
        for k in range(nchunk):
            P.add("pe", lambda e, k=k: e.transpose(out=pT[:, k * 128:(k + 1) * 128], in_=hb[:, k * 128:(k + 1) * 128],
                                                    identity=self.idb), r=[hbkey, "idb"], w=[pk])
        P.add("act", lambda e: e.copy(out=hT[:, 0:nchunk, col0:col0 + 128],
                                       in_=pT[:, 0:nchunk * 128].rearrange("p (k n) -> p k n", k=nchunk)),
              r=[pk], w=[hTkey])

    def ffn(self, L, half, xn, xa, xo, final_g=None):
        nc, P, A = self.nc, self.P, self.A
        P.barrier()
        A.reset(self.base_mark)
        TB = 512
        nblk = self.ntok // TB
        w1 = A.bf16(8 * 2048).rearrange("p (k n) -> p k n", k=8)
        w2 = A.bf16(16 * 1024).rearrange("p (k n) -> p k n", k=16)
        gt = A.f32(1024)
        w1d = self.dram["ffn_w1"][L][:, half * 2048:(half + 1) * 2048].rearrange("(k p) n -> p k n", p=128)
        w2d = self.dram["ffn_w2"][L][half * 2048:(half + 1) * 2048, :].rearrange("(k p) n -> p k n", p=128)
        self.load_w(w1, w1d, "w1", 4)
        self.load_w(w2, w2d, "w2", 2)
        self.load_bcast(gt, self.dram["ffn_norm_g"][L:L + 1, :], "gt")
        same = xa is xn
        xt = [[A.f32(1024) for t in range(4)] for s in range(2)]
        xat = xt if same else [[A.f32(1024) for t in range(4)] for s in range(2)]
        hb = [A.bf16(1024) for s in range(2)]
        hT = [A.bf16(8 * TB).rearrange("p (k n) -> p k n", k=8) for s in range(2)]
        scr = [dict(junk=A.f32(1024), ss=A.f32(1), rs=A.f32(1), key="scr%d" % s) for s in range(2)]
        rr = [A.f32(512) for s in range(2)]
        uT = A.bf16(16 * TB).rearrange("p (k n) -> p k n", k=16)
        xot = [A.f32(1024) for s in range(2)]

        def prologue(b):
            s = b % 2
            for t in range(4):
                row = b * TB + t * 128
                P.add("sp", lambda e, t=t, row=row: e.dma_start(out=xt[s][t], in_=xn[row:row + 128, :]),
                      w=[("xt", s, t)], dma=("xt", s, t))
                if not same:
                    P.add("sp", lambda e, t=t, row=row: e.dma_start(out=xat[s][t], in_=xa[row:row + 128, :]),
                          w=[("xat", s, t)], dma=("xat", s, t))
                q = t % 2
                self.norm_tile(xt[s][t], ("xt", s, t), gt, "gt", hb[q], ("hb", q), scr[q])
                self.transpose_tile(hb[q], ("hb", q), hT[s], ("hT", s), t * 128, 0)

        def ffn1(b):
            s = b % 2
            for m in range(16):
                bk = 1 + m % 2
                for k in range(8):
                    P.add("pe", lambda e, m=m, k=k, bk=bk: e.matmul(self.bank(bk), lhsT=w1[:, k, m * 128:(m + 1) * 128],
                                                                   rhs=hT[s][:, k, :], start=(k == 0), stop=(k == 7)),
                          r=self.wk("w1", m * 128, 128) + [("hT", s)], w=[("ps", bk)])
                q = m % 2
                P.add("act", lambda e, bk=bk, q=q: e.activation(out=rr[q], in_=self.bank(bk), func=AF.Relu),
                      r=[("ps", bk)], w=[("rr", q)])
                P.add("dve", lambda e, m=m, q=q: e.tensor_tensor(out=uT[:, m, :], in0=rr[q], in1=rr[q], op=ALU.mult),
                      r=[("rr", q)], w=[("uT", m)])

        def ffn2(b):
            s = b % 2
            for t in range(4):
                row = b * TB + t * 128
                q = t % 2
                for nh in range(2):
                    bk = 3 + nh
                    for m in range(16):
                        P.add("pe", lambda e, m=m, t=t, nh=nh, bk=bk: e.matmul(
                            self.bank(bk), lhsT=uT[:, m, t * 128:(t + 1) * 128], rhs=w2[:, m, nh * 512:(nh + 1) * 512],
                            start=(m == 0), stop=(m == 15)), r=[("uT", m)] + self.wk("w2", nh * 512, 512), w=[("ps", bk)])
                    P.add("dve", lambda e, t=t, nh=nh, bk=bk, q=q: e.tensor_tensor(
                        out=xot[q][:, nh * 512:(nh + 1) * 512], in0=self.bank(bk), in1=xat[s][t][:, nh * 512:(nh + 1) * 512],
                        op=ALU.add), r=[("ps", bk), ("xt" if same else "xat", s, t)], w=[("xot", q)])
                P.add("sp", lambda e, row=row, q=q: e.dma_start(out=xo[row:row + 128, :], in_=xot[q]),
                      r=[("xot", q)], w=[("dram", id(xo), row)], dma=("xot", q))

        prologue(0)
        for b in range(nblk):
            ffn1(b)
            if b + 1 < nblk:
                prologue(b + 1)
            ffn2(b)

    def final(self, xn, out):
        P, A = self.P, self.A
        P.barrier()
        A.reset(self.base_mark)
        gt = A.f32(1024)
        self.load_bcast(gt, self.dram["final_norm_g"].rearrange("(o n) -> o n", o=1), "gt")
        xt = [A.f32(1024) for s in range(2)]
        ot = [A.f32(1024) for s in range(2)]
        scr = [dict(junk=A.f32(1024), ss=A.f32(1), rs=A.f32(1), key="scr%d" % s) for s in range(2)]
        for i in range(self.ntok // 128):
            s = i % 2
            row = i * 128
            P.add("sp", lambda e, s=s, row=row: e.dma_start(out=xt[s], in_=xn[row:row + 128, :]), w=[("xt", s)], dma=("xt", s))
            self.norm_tile(xt[s], ("xt", s), gt, "gt", ot[s], ("ot", s), scr[s])
            P.add("sp", lambda e, s=s, row=row: e.dma_start(out=out[row:row + 128, :], in_=ot[s]), r=[("ot", s)],
                  w=[("dram", "out", row)], dma=("ot", s))


PARAM_SHAPES = {
    "even_w_in": (2, 1024, 4624), "even_w_out": (2, 2048, 1024), "ssd_conv_w": (2, 4, 1536), "ssd_conv_b": (2, 1536),
    "ssd_dt_bias": (2, 16), "ssd_a_log": (2, 16), "ssd_d": (2, 16), "ssd_norm_g": (2, 1024), "gmlp_ln_g": (2, 1024),
    "gmlp_ln_b": (2, 1024), "gmlp_w_s": (2, 8, 128, 128), "gmlp_b_s": (2, 8, 128), "odd_w_in": (2, 1024, 6144),
    "odd_w_out": (2, 2048, 1024), "conf_conv_w": (2, 31, 1024), "conf_conv_b": (2, 1024), "conf_ln_g": (2, 1024),
    "conf_ln_b": (2, 1024), "hgrn_lb_logits": (2, 1024), "hgrn_norm_g": (2, 1024), "mix_norm_g": (4, 1024),
    "ffn_norm_g": (4, 1024), "ffn_w1": (4, 1024, 4096), "ffn_w2": (4, 4096, 1024), "final_norm_g": (1024,),
}


def make_consts():
    c = {}
    c["c_ident"] = np.eye(128, dtype=np.float32)
    c["c_maskT"] = np.triu(np.ones((128, 128), dtype=np.float32))
    rm = np.ones((128, 512), dtype=np.float32)
    rm[:, ::64] = 0.0
    c["c_rmask"] = rm
    c["c_strict"] = np.tril(np.ones((128, 128), dtype=np.float32), -1)
    si = np.arange(128)[:, None]
    ti = np.arange(128)[None, :]
    am = ((si // 64 == ti // 64) & (si <= ti)).astype(np.float32)
    c["c_amask4"] = np.tile(am, (1, 4))
    return c


DEBUG = False
SCHEDULE = True
FFN_MERGED = True
FUSE_FINAL = True
PRIO_CP = True
ACT_TABLE_AWARE = True


def build(ntok, seq, plan):
    B = Builder(ntok, seq, None)
    B.debug = DEBUG
    nc = B.nc
    x_in = B.din("x", (ntok, D))
    for k, shp in PARAM_SHAPES.items():
        B.din(k, shp)
    for k, v in make_consts().items():
        B.din(k, v.shape)
    out = nc.dram_tensor("out", [ntok, D], F32, kind="ExternalOutput").ap()
    scr = [nc.dram_tensor("xs%d" % i, [ntok, D], F32).ap() for i in range(3)]
    with ExitStack() as es:
        B.setup(es)
        cur = x_in

        def free(*used):
            for s in scr:
                if all(s is not u for u in used):
                    return s

        skip_final = False
        for pi, ph in enumerate(plan):
            kind = ph[0]
            if kind == "final" and skip_final:
                continue
            if kind == "ffn":
                L = ph[1]
                if FFN_MERGED:
                    if FUSE_FINAL and pi + 1 < len(plan) and plan[pi + 1][0] == "final":
                        B.ffn_full(L, cur, None, final_out=out)
                        skip_final = True
                        continue
                    b1 = free(cur)
                    B.ffn_full(L, cur, b1)
                    cur = b1
                else:
                    b1 = free(cur)
                    B.ffn(L, 0, cur, cur, b1)
                    b2 = free(cur, b1)
                    B.ffn(L, 1, cur, b1, b2)
                    cur = b2
            elif kind in ("ssd", "gmlp", "conf", "hgrn"):
                L, xn = ph[1], ph[2]
                if xn is None:
                    b1 = free(cur)
                    getattr(B, kind)(L, cur, cur, b1)
                    B.mix_src = cur
                    cur = b1
                else:
                    b2 = free(cur, B.mix_src)
                    getattr(B, kind)(L, B.mix_src, cur, b2)
                    cur = b2
            elif kind == "final":
                B.final(cur, out)
            elif kind == "copy":
                B.P.barrier()
                B.P.add("sp", lambda e, cur=cur: e.dma_start(out=out, in_=cur), w=["out"], dma="outcopy")
        if SCHEDULE:
            B.P.schedule()
        B.P.emit(nc)
    return nc


def full_plan():
    plan = []
    for L in range(4):
        if L % 2 == 0:
            plan += [("ssd", L, None), ("gmlp", L, 1)]
        else:
            plan += [("conf", L, None), ("hgrn", L, 1)]
        plan.append(("ffn", L))
    plan.append(("final",))
    return plan


def run(x, params, plan, ncores=8, seq=None):
    bsz, s, _ = x.shape
    per = bsz // ncores
    ntok = per * s
    nc = build(ntok, s, plan)
    consts = make_consts()
    in_maps = []
    for c in range(ncores):
        m = {"x": np.ascontiguousarray(x[c * per:(c + 1) * per].reshape(ntok, D))}
        m.update(params)
        m.update(consts)
        in_maps.append(m)
    res = bu.run_bass_kernel_spmd(nc, in_maps, core_ids=list(range(ncores)))
    return np.concatenate([r["out"].reshape(per, s, D) for r in res.results], axis=0), res


def kernel(**inputs):
    x = np.ascontiguousarray(inputs["x"], dtype=np.float32)
    params = {k: np.ascontiguousarray(inputs[k], dtype=np.float32) for k in PARAM_SHAPES}
    out, _ = run(x, params, full_plan(), ncores=8)
    return out.astype(np.float32)


TB = 512


def _alloc_blk(self, same, nslot=1, junk=None):
    A = self.A
    st = {}
    nx = 4 if same else 2
    st["nx"] = nx
    st["xt"] = [[A.f32(1024) for t in range(nx)] for s in range(nslot)]
    st["xat"] = st["xt"] if same else [[A.f32(1024) for t in range(4)] for s in range(nslot)]
    st["xak"] = "xt" if same else "xat"
    st["hb"] = [A.bf16(1024) for s in range(2)]
    st["hT"] = [A.bf16(8 * TB).rearrange("p (k n) -> p k n", k=8) for s in range(nslot)]
    if junk is None:
        junk = (A.f32(1024), "junk")
    st["scr"] = [dict(junk=junk[0], jkey=junk[1], ss=A.f32(1), rs=A.f32(1), key="scr%d" % s) for s in range(2)]
    st["same"] = same
    st["nslot"] = nslot
    return st


def _prologue(self, b, st, xn, xa, gt):
    P = self.P
    s = b % st["nslot"]
    nx = st["nx"]
    for t in range(4):
        row = b * TB + t * 128
        xb = st["xt"][s][t % nx]
        xk = ("xt", s, t % nx)
        P.add("sp", lambda e, xb=xb, row=row: e.dma_start(out=xb, in_=xn[row:row + 128, :]), w=[xk], dma=xk)
        if not st["same"]:
            P.add("sp", lambda e, t=t, row=row: e.dma_start(out=st["xat"][s][t], in_=xa[row:row + 128, :]),
                  w=[("xat", s, t)], dma=("xat", s, t))
        q = t % 2
        self.norm_tile(xb, xk, gt, "gt", st["hb"][q], ("hb", q), st["scr"][q])
        self.transpose_tile(st["hb"][q], ("hb", q), st["hT"][s], ("hT", s), t * 128, 0)


def _mm_fm(self, bk, W, wkey, c0, hT, hTkey):
    for k in range(8):
        self.P.add("pe", lambda e, k=k: e.matmul(self.bank(bk), lhsT=W[:, k, c0:c0 + 128], rhs=hT[:, k, :],
                                                 start=(k == 0), stop=(k == 7)), r=self.wk(wkey, c0, 128) + [hTkey], w=[("ps", bk)])


def _mm_tm(self, bk, hT, hTkey, t, W, wkey, c0, n=512):
    for k in range(8):
        self.P.add("pe", lambda e, k=k: e.matmul(self.bank(bk)[:, 0:n], lhsT=hT[:, k, t * 128:(t + 1) * 128],
                                                 rhs=W[:, k, c0:c0 + n], start=(k == 0), stop=(k == 7)),
                   r=self.wk(wkey, c0, n) + [hTkey], w=[("ps", bk)])


def _out_proj(self, b, st, mixT, mkey, wo, xo, kc=8):
    P = self.P
    s = b % st["nslot"]
    for t in range(4):
        row = b * TB + t * 128
        q = t % 2
        for nh in range(2):
            bk = 7 if nh == 0 else 0
            for k in range(kc):
                P.add("pe", lambda e, k=k, t=t, nh=nh, bk=bk: e.matmul(
                    self.bank(bk), lhsT=mixT[:, k, t * 128:(t + 1) * 128], rhs=wo[:, k, nh * 512:(nh + 1) * 512],
                    start=(k == 0), stop=(k == kc - 1)), r=[mkey] + self.wk("wo", nh * 512, 512), w=[("ps", bk)])
            P.add("dve", lambda e, t=t, nh=nh, bk=bk, q=q: e.tensor_tensor(
                out=st["xat"][s][t][:, nh * 512:(nh + 1) * 512], in0=self.bank(bk),
                in1=st["xat"][s][t][:, nh * 512:(nh + 1) * 512], op=ALU.add),
                r=[("ps", bk), (st["xak"], s, t)], w=[(st["xak"], s, t)])
        P.add("sp", lambda e, row=row, t=t: e.dma_start(out=xo[row:row + 128, :], in_=st["xat"][s][t]),
              r=[(st["xak"], s, t)], w=[("dram", id(xo), row)], dma=("xst", s, t))


def _load_cols(self, dst, src_rows, n, key, nch=8, bk=0):
    P, A = self.P, self.A
    rows = A.f32(nch * 128)
    rk = key + "_rows"
    P.add("sp", lambda e: e.dma_start(out=rows[0:n, :], in_=src_rows), w=[rk], dma=rk)
    for m in range(nch):
        P.add("pe", lambda e, m=m: e.matmul(self.bank(bk)[:, m * n:(m + 1) * n], lhsT=rows[0:n, m * 128:(m + 1) * 128],
                                            rhs=self.idf[0:n, 0:n], start=True, stop=True),
              r=[rk, "idf"], w=[("ps", bk)])
    P.add("dve", lambda e: e.tensor_copy(out=dst, in_=self.bank(bk)[:, 0:nch * n].rearrange("p (m j) -> p m j", m=nch)),
          r=[("ps", bk)], w=[key])


def _w_in_view(self, name, idx, c0, n):
    return self.dram[name][idx][:, c0:c0 + n].rearrange("(k p) n -> p k n", p=128)


def _gmlp(self, L, xn, xa, xo):
    P, A = self.P, self.A
    e_ = L // 2
    P.barrier()
    A.reset(self.base_mark)
    C0 = 2576
    wu = A.bf16(8 * 1024).rearrange("p (k n) -> p k n", k=8)
    wv = A.bf16(8 * 1024).rearrange("p (k n) -> p k n", k=8)
    wo = A.bf16(8 * 1024).rearrange("p (k n) -> p k n", k=8)
    self.load_w(wu, _w_in_view(self, "even_w_in", e_, C0, 1024), "wu", 4)
    self.load_w(wv, _w_in_view(self, "even_w_in", e_, C0 + 1024, 1024), "wv", 4)
    self.load_w(wo, self.dram["even_w_out"][e_][1024:2048, :].rearrange("(k p) n -> p k n", p=128), "wo", 4)
    gt = A.f32(1024)
    lg = A.f32(1024)
    lb = A.f32(1024)
    bs = A.f32(1024)
    self.load_bcast(gt, self.dram["mix_norm_g"][L:L + 1, :], "gt")
    self.load_bcast(lg, self.dram["gmlp_ln_g"][e_:e_ + 1, :], "lg")
    self.load_bcast(lb, self.dram["gmlp_ln_b"][e_:e_ + 1, :], "lb")
    self.load_bcast(bs, self.dram["gmlp_b_s"][e_:e_ + 1].rearrange("o g t -> o (g t)"), "bs")
    cm = A.f32(128)
    P.add("sp", lambda e: e.dma_start(out=cm, in_=self.dram["c_maskT"]), w=["cm"], dma="cm")
    wsT = A.bf16(1024).rearrange("p (g t) -> p g t", g=8)
    mk = A.mark()
    wsf = A.f32(1024)
    wsb = A.bf16(1024)
    P.add("sp", lambda e: e.dma_start(out=wsf.rearrange("p (g s) -> p g s", g=8),
                                      in_=self.dram["gmlp_w_s"][e_].rearrange("g t s -> t g s")), w=["wsf"], dma="wsf")
    P.add("dve", lambda e: e.tensor_copy(out=wsb, in_=wsf), r=["wsf"], w=["wsb"])
    pT = self.bank_bf(0)
    for g in range(8):
        P.add("pe", lambda e, g=g: e.transpose(out=pT[:, g * 128:(g + 1) * 128], in_=wsb[:, g * 128:(g + 1) * 128],
                                                identity=self.idb), r=["wsb", "idb"], w=[("ps", 0)])
    cmb = A.bf16(128)
    P.add("dve", lambda e: e.tensor_copy(out=cmb, in_=cm), r=["cm"], w=["cmb"])
    P.add("act", lambda e: e.copy(out=wsT.rearrange("p g t -> p (g t)"), in_=pT), r=[("ps", 0)], w=["wsT"])
    for g in range(8):
        P.add("dve", lambda e, g=g: e.tensor_tensor(out=wsT[:, g, :], in0=wsT[:, g, :], in1=cmb, op=ALU.mult),
              r=["wsT", "cmb"], w=["wsT"])
    P.barrier(keep=("wu", "wv", "wo"))
    A.reset(mk)
    st = _alloc_blk(self, xa is xn)
    u = A.f32(8 * TB).rearrange("p (k n) -> p k n", k=8)
    v = [A.f32(1024) for i in range(2)]
    junk = A.f32(1024)
    vnb = [A.bf16(1024) for i in range(2)]
    stt = [A.f32(8) for i in range(2)]
    tmp = [A.f32(512) for i in range(2)]
    mixT = A.bf16(8 * TB).rearrange("p (k n) -> p k n", k=8)
    for b in range(self.ntok // TB):
        _prologue(self, b, st, xn, xa, gt)
        hT = st["hT"][0]
        hk = ("hT", 0)
        for m in range(8):
            bk = 1 + m % 2
            _mm_fm(self, bk, wu, "wu", m * 128, hT, hk)
            P.add("act", lambda e, m=m, bk=bk: e.activation(out=u[:, m, :], in_=self.bank(bk), func=AF.Gelu),
                  r=[("ps", bk)], w=[("u", m)])
        for t in range(4):
            q = t % 2
            S = stt[q]
            sk = ("stt", q)
            for nh in range(2):
                bk = 3 + nh
                _mm_tm(self, bk, hT, hk, t, wv, "wv", nh * 512)
                P.add("act", lambda e, nh=nh, bk=bk, q=q, S=S: e.activation(
                    out=v[q][:, nh * 512:(nh + 1) * 512], in_=self.bank(bk), func=AF.Gelu, accum_out=S[:, nh:nh + 1]),
                    r=[("ps", bk)], w=[("v", q), sk])
            P.add("act", lambda e, q=q, S=S: e.activation(out=junk, in_=v[q], func=AF.Square, accum_out=S[:, 2:3]),
                  r=[("v", q)], w=["junk", sk])
            P.add("dve", lambda e, S=S: e.tensor_tensor(out=S[:, 3:4], in0=S[:, 0:1], in1=S[:, 1:2], op=ALU.add), r=[sk], w=[sk])
            P.add("dve", lambda e, S=S: e.tensor_scalar(out=S[:, 3:4], in0=S[:, 3:4], scalar1=1.0 / 1024, scalar2=0.0,
                                                       op0=ALU.mult, op1=ALU.add), r=[sk], w=[sk])
            P.add("dve", lambda e, S=S: e.tensor_tensor(out=S[:, 4:5], in0=S[:, 3:4], in1=S[:, 3:4], op=ALU.mult), r=[sk], w=[sk])
            P.add("dve", lambda e, S=S: e.scalar_tensor_tensor(out=S[:, 4:5], in0=S[:, 2:3], scalar=1.0 / 1024, in1=S[:, 4:5],
                                                              op0=ALU.mult, op1=ALU.subtract), r=[sk], w=[sk])
            P.add("dve", lambda e, S=S: e.tensor_scalar(out=S[:, 4:5], in0=S[:, 4:5], scalar1=EPS, scalar2=0.0, op0=ALU.add,
                                                       op1=ALU.add), r=[sk], w=[sk])
            P.add("act", lambda e, S=S: e.activation(out=S[:, 5:6], in_=S[:, 4:5], func=AF.Sqrt), r=[sk], w=[sk])
            P.add("dve", lambda e, S=S: e.reciprocal(out=S[:, 5:6], in_=S[:, 5:6]), r=[sk], w=[sk])
            P.add("dve", lambda e, q=q, S=S: e.tensor_scalar(out=v[q], in0=v[q], scalar1=S[:, 3:4], scalar2=S[:, 5:6],
                                                            op0=ALU.subtract, op1=ALU.mult), r=[("v", q), sk], w=[("v", q)])
            P.add("dve", lambda e, q=q: e.tensor_tensor(out=v[q], in0=v[q], in1=lg, op=ALU.mult), r=[("v", q), "lg"], w=[("v", q)])
            P.add("dve", lambda e, q=q: e.tensor_tensor(out=vnb[q], in0=v[q], in1=lb, op=ALU.add), r=[("v", q), "lb"],
                  w=[("vnb", q)])
            for g in range(8):
                bk = 5 + g // 4
                P.add("pe", lambda e, g=g, bk=bk, q=q: e.matmul(self.bank(bk)[:, (g % 4) * 128:(g % 4 + 1) * 128],
                                                               lhsT=vnb[q][:, g * 128:(g + 1) * 128], rhs=wsT[:, g, :],
                                                               start=True, stop=True), r=[("vnb", q), "wsT"], w=[("ps", bk)])
            for hf in range(2):
                bk = 5 + hf
                P.add("dve", lambda e, hf=hf, bk=bk: e.tensor_tensor(out=tmp[hf], in0=self.bank(bk),
                                                                    in1=bs[:, hf * 512:(hf + 1) * 512], op=ALU.add),
                      r=[("ps", bk), "bs"], w=[("tmp", hf)])
                P.add("dve", lambda e, hf=hf, t=t: e.tensor_tensor(
                    out=mixT[:, hf * 4:(hf + 1) * 4, t * 128:(t + 1) * 128],
                    in0=tmp[hf].rearrange("p (g n) -> p g n", g=4), in1=u[:, hf * 4:(hf + 1) * 4, t * 128:(t + 1) * 128],
                    op=ALU.mult), r=[("tmp", hf)] + [("u", hf * 4 + i) for i in range(4)], w=["mixT"])
        self.dump("hT", hT, [hk])
        self.dump("u", u, [("u", m) for m in range(8)])
        self.dump("v", v[1], [("v", 1)])
        self.dump("S", stt[1], [("stt", 1)])
        self.dump("vnb", vnb[1], [("vnb", 1)])
        self.dump("wsT", wsT, ["wsT"])
        self.dump("tmp", tmp[1], [("tmp", 1)])
        self.dump("mixT", mixT, ["mixT"])
        _out_proj(self, b, st, mixT, "mixT", wo, xo)


Builder.gmlp = _gmlp


def _conf(self, L, xn, xa, xo):
    P, A = self.P, self.A
    o_ = L // 2
    P.barrier()
    A.reset(self.base_mark)
    wa = A.bf16(8 * 1024).rearrange("p (k n) -> p k n", k=8)
    wg = A.bf16(8 * 1024).rearrange("p (k n) -> p k n", k=8)
    wo = A.bf16(8 * 1024).rearrange("p (k n) -> p k n", k=8)
    self.load_w(wa, _w_in_view(self, "odd_w_in", o_, 0, 1024), "wa", 4)
    self.load_w(wg, _w_in_view(self, "odd_w_in", o_, 1024, 1024), "wg", 4)
    self.load_w(wo, self.dram["odd_w_out"][o_][0:1024, :].rearrange("(k p) n -> p k n", p=128), "wo", 4)
    gt = A.f32(1024)
    self.load_bcast(gt, self.dram["mix_norm_g"][L:L + 1, :], "gt")
    cw = A.f32(8 * 31).rearrange("p (m j) -> p m j", m=8)
    cv = A.f32(8 * 3).rearrange("p (m j) -> p m j", m=8)
    diag = A.bf16(8 * 31 * 128).rearrange("p (m j c) -> p m j c", m=8, j=31)
    mk = A.mark()
    _load_cols(self, cw, self.dram["conf_conv_w"][o_], 31, "cw")
    rows3 = A.f32(1024)
    for j, nm in enumerate(["conf_conv_b", "conf_ln_g", "conf_ln_b"]):
        P.add("sp", lambda e, j=j, nm=nm: e.dma_start(out=rows3[j:j + 1, :], in_=self.dram[nm][o_:o_ + 1, :]),
              w=["rows3"], dma="rows3")
    for m in range(8):
        P.add("pe", lambda e, m=m: e.matmul(self.bank(1)[:, m * 3:(m + 1) * 3], lhsT=rows3[0:3, m * 128:(m + 1) * 128],
                                            rhs=self.idf[0:3, 0:3], start=True, stop=True),
              r=["rows3", "idf"], w=[("ps", 1)])
    P.add("dve", lambda e: e.tensor_copy(out=cv, in_=self.bank(1)[:, 0:24].rearrange("p (m j) -> p m j", m=8)),
          r=[("ps", 1)], w=["cv"])
    for m in range(8):
        for j in range(31):
            eng = "dve"
            P.add(eng, lambda e, m=m, j=j: e.scalar_tensor_tensor(out=diag[:, m, j, :], in0=self.idf, scalar=cw[:, m, j:j + 1],
                                                                in1=self.idf, op0=ALU.mult, op1=ALU.mult),
                  r=["idf", "cw"], w=[("diag", m, j)])
    self.dump("cw", cw, ["cw"])
    self.dump("cv", cv, ["cv"])
    self.dump("diag0", diag[:, 0, :, :], [("diag", 0, j) for j in range(31)])
    P.barrier(keep=("wa", "wg", "wo"))
    A.reset(mk)
    st = _alloc_blk(self, xa is xn)
    HAL = 30
    cin1 = A.bf16(8 * (TB + HAL)).rearrange("p (m n) -> p m n", m=8)
    cin = [cin1, cin1]
    sg = [A.f32(TB) for i in range(2)]
    hc = A.f32(8 * TB).rearrange("p (m n) -> p m n", m=8)
    hcb = [A.bf16(TB) for i in range(2)]
    hsq = [A.bf16(TB) for i in range(2)]
    mean = A.f32(TB)
    rstd = A.f32(TB)
    tq = [A.f32(TB) for i in range(2)]
    mixT = A.bf16(8 * TB).rearrange("p (k n) -> p k n", k=8)
    nblk_seq = self.seq // TB
    for b in range(self.ntok // TB):
        s = 0
        _prologue(self, b, st, xn, xa, gt)
        hT = st["hT"][0]
        hk = ("hT", 0)
        if b % nblk_seq == 0:
            P.add("pool", lambda e, s=s: e.memset(cin[s][:, :, 0:HAL], 0.0), w=[("cin", s, m) for m in range(8)])
        else:
            P.add("pool", lambda e, s=s: e.tensor_copy(out=cin[s][:, :, 0:HAL], in_=cin[s][:, :, TB:TB + HAL]),
                  r=[("cin", s, m) for m in range(8)], w=[("cin", s, m) for m in range(8)])
        for m in range(8):
            _mm_fm(self, 1, wa, "wa", m * 128, hT, hk)
            _mm_fm(self, 2, wg, "wg", m * 128, hT, hk)
            q = m % 2
            P.add("act", lambda e, q=q: e.activation(out=sg[q], in_=self.bank(2), func=AF.Sigmoid), r=[("ps", 2)], w=[("sg", q)])
            P.add("dve", lambda e, q=q, m=m, s=s: e.tensor_tensor(out=cin[s][:, m, HAL:HAL + TB], in0=self.bank(1), in1=sg[q],
                                                                 op=ALU.mult), r=[("ps", 1), ("sg", q)], w=[("cin", s, m)])
        for m in range(8):
            bk = 3 + m % 2
            q = m % 2
            for j in range(31):
                P.add("pe", lambda e, m=m, j=j, bk=bk, s=s: e.matmul(self.bank(bk), lhsT=diag[:, m, j, :],
                                                                    rhs=cin[s][:, m, j:j + TB], start=(j == 0), stop=(j == 30)),
                      r=[("diag", m, j), ("cin", s, m)], w=[("ps", bk)])
            P.add("act", lambda e, m=m, bk=bk: e.activation(out=hc[:, m, :], in_=self.bank(bk), func=AF.Identity,
                                                           bias=cv[:, m, 0:1], scale=1.0), r=[("ps", bk), "cv"], w=[("hc", m)])
            P.add("dve", lambda e, m=m, q=q: e.tensor_copy(out=hcb[q], in_=hc[:, m, :]), r=[("hc", m)], w=[("hcb", q)])
            P.add("act", lambda e, m=m, q=q: e.activation(out=hsq[q], in_=hc[:, m, :], func=AF.Square), r=[("hc", m)],
                  w=[("hsq", q)])
            P.add("pe", lambda e, m=m, q=q: e.matmul(self.bank(5), lhsT=self.onesq,
                                                    rhs=hcb[q], start=(m == 0), stop=(m == 7)), r=[("hcb", q), "onesq"], w=[("ps", 5)])
            P.add("pe", lambda e, m=m, q=q: e.matmul(self.bank(6), lhsT=self.onesq, rhs=hsq[q], start=(m == 0), stop=(m == 7)),
                  r=[("hsq", q), "onesq"], w=[("ps", 6)])
        P.add("act", lambda e: e.activation(out=mean, in_=self.bank(5), func=AF.Copy, scale=1.0 / 1024), r=[("ps", 5)], w=["mean"])
        P.add("dve", lambda e: e.tensor_tensor(out=rstd, in0=mean, in1=mean, op=ALU.mult), r=["mean"], w=["rstd"])
        P.add("dve", lambda e: e.scalar_tensor_tensor(out=rstd, in0=self.bank(6), scalar=1.0 / 1024, in1=rstd, op0=ALU.mult,
                                                      op1=ALU.subtract), r=[("ps", 6), "rstd"], w=["rstd"])
        P.add("dve", lambda e: e.tensor_scalar(out=rstd, in0=rstd, scalar1=EPS, scalar2=0.0, op0=ALU.add, op1=ALU.add),
              r=["rstd"], w=["rstd"])
        P.add("act", lambda e: e.activation(out=rstd, in_=rstd, func=AF.Sqrt), r=["rstd"], w=["rstd"])
        P.add("dve", lambda e: e.reciprocal(out=rstd, in_=rstd), r=["rstd"], w=["rstd"])
        for m in range(8):
            q = m % 2
            P.add("dve", lambda e, m=m, q=q: e.tensor_tensor(out=tq[q], in0=hc[:, m, :], in1=mean, op=ALU.subtract),
                  r=[("hc", m), "mean"], w=[("tq", q)])
            P.add("dve", lambda e, q=q: e.tensor_tensor(out=tq[q], in0=tq[q], in1=rstd, op=ALU.mult), r=[("tq", q), "rstd"],
                  w=[("tq", q)])
            P.add("act", lambda e, m=m, q=q: e.activation(out=mixT[:, m, :], in_=tq[q], func=AF.Silu, scale=cv[:, m, 1:2],
                                                         bias=cv[:, m, 2:3]), r=[("tq", q), "cv"], w=["mixT"])
        self.dump("cin", cin[s], [("cin", s, m) for m in range(8)])
        self.dump("hc", hc, [("hc", m) for m in range(8)])
        self.dump("mean", mean, ["mean"])
        self.dump("rstd", rstd, ["rstd"])
        self.dump("mixT", mixT, ["mixT"])
        _out_proj(self, b, st, mixT, "mixT", wo, xo)


Builder.conf = _conf


def _hgrn(self, L, xn, xa, xo):
    P, A = self.P, self.A
    o_ = L // 2
    U32 = mybir.dt.uint32
    P.barrier()
    A.reset(self.base_mark)
    W = {}
    for i, nm in [(1, "wf"), (0, "wq"), (2, "wi"), (3, "wz")]:
        W[nm] = A.bf16(8 * 1024).rearrange("p (k n) -> p k n", k=8)
        self.load_w(W[nm], _w_in_view(self, "odd_w_in", o_, 2048 + i * 1024, 1024), nm, 4)
    wo = A.bf16(8 * 1024).rearrange("p (k n) -> p k n", k=8)
    self.load_w(wo, self.dram["odd_w_out"][o_][1024:2048, :].rearrange("(k p) n -> p k n", p=128), "wo", 4)
    gt = A.f32(1024)
    hg = A.f32(1024)
    self.load_bcast(gt, self.dram["mix_norm_g"][L:L + 1, :], "gt")
    self.load_bcast(hg, self.dram["hgrn_norm_g"][o_:o_ + 1, :], "hg")
    rmask = A.f32(512)
    amask = A.f32(512)
    P.add("sp", lambda e: e.dma_start(out=rmask, in_=self.dram["c_rmask"]), w=["rmask"], dma="rmask")
    P.add("sp", lambda e: e.dma_start(out=amask, in_=self.dram["c_amask4"]), w=["amask"], dma="amask")
    lbc = A.f32(8)
    omlb = A.f32(8)
    mk = A.mark()
    if o_ == 0:
        P.add("dve", lambda e: e.memset(lbc, 0.0), w=["lbc"])
    else:
        lgc = A.f32(16).rearrange("p (m j) -> p m j", m=8)
        _load_cols(self, lgc, self.dram["hgrn_lb_logits"], 2, "lgc")
        P.add("dve", lambda e: e.tensor_tensor(out=lbc, in0=lgc[:, :, 1], in1=lgc[:, :, 0], op=ALU.subtract), r=["lgc"], w=["lbc"])
        P.add("act", lambda e: e.activation(out=lbc, in_=lbc, func=AF.Sigmoid), r=["lbc"], w=["lbc"])
    P.add("dve", lambda e: e.tensor_scalar(out=omlb, in0=lbc, scalar1=-1.0, scalar2=1.0, op0=ALU.mult, op1=ALU.add),
          r=["lbc"], w=["omlb"])
    P.barrier(keep=("wq", "wf", "wi", "wz", "wo"))
    A.reset(mk)
    on = A.f32(1024)
    st = _alloc_blk(self, xa is xn, junk=(on, "on"))
    qg = A.bf16(8 * TB).rearrange("p (h n) -> p h n", h=8)
    qd = A.bf16(8 * TB).rearrange("p (h n) -> p h n", h=8)
    kg = A.bf16(8 * TB).rearrange("p (h n) -> p h n", h=8)
    kdf = A.bf16(TB)
    kdT = A.bf16(4 * 8 * 128).rearrange("p (t h k) -> p t h k", t=4, h=8)
    ivb = A.bf16(4 * 1024).rearrange("p (t n) -> p t n", t=4)
    sgt1 = A.f32(1024)
    sgt = [sgt1, sgt1]
    ff = A.f32(TB)
    logf = A.f32(TB)
    kk = A.f32(TB)
    bcum = A.f32(TB)
    d1 = A.f32(TB)
    eqg = A.f32(TB)
    ekg = A.f32(TB)
    eqd = ff
    ekd = logf
    ecd = A.f32(64).rearrange("p (h c) -> p h c", h=8)
    S = A.f32(1024)
    Sb = [A.bf16(1024) for i in range(2)]
    attb = [A.bf16(1024) for i in range(2)]
    r8 = A.f32(8)
    osq = on
    yb = A.bf16(1024)
    mixT = st["hT"][0]
    MK = ("hT", 0)
    for i in range(2):
        P.add("pool", lambda e, i=i: e.memset(attb[i], 0.0), w=[("attb", i)])
    SC = float(128 ** -0.5)
    nblk_seq = self.seq // TB
    sidx = 0
    c3 = lambda ap: ap.rearrange("p (c j) -> p c j", j=64)
    for b in range(self.ntok // TB):
        _prologue(self, b, st, xn, xa, gt)
        hT = st["hT"][0]
        hk = ("hT", 0)
        if b % nblk_seq == 0:
            P.add("pool", lambda e: e.memset(S, 0.0), w=["S"])
            P.add("pool", lambda e, sidx=sidx: e.memset(Sb[sidx], 0.0), w=[("Sb", sidx)])
        for h in range(8):
            _mm_fm(self, 1, W["wf"], "wf", h * 128, hT, hk)
            P.add("act", lambda e: e.activation(out=ff, in_=self.bank(1), func=AF.Sigmoid), r=[("ps", 1)], w=["ff"])
            P.add("dve", lambda e, h=h: e.tensor_scalar(out=ff, in0=ff, scalar1=omlb[:, h:h + 1], scalar2=lbc[:, h:h + 1],
                                                       op0=ALU.mult, op1=ALU.add), r=["ff", "omlb", "lbc"], w=["ff"])
            P.add("act", lambda e: e.activation(out=logf, in_=ff, func=AF.Ln), r=["ff"], w=["logf"])
            P.add("dve", lambda e: e.tensor_scalar(out=kk, in0=ff, scalar1=-1.0, scalar2=1.0, op0=ALU.mult, op1=ALU.add),
                  r=["ff"], w=["kk"])
            P.add("dve", lambda e: e.tensor_tensor_scan(out=bcum, data0=rmask, data1=logf, initial=0.0, op0=ALU.mult,
                                                        op1=ALU.add), r=["rmask", "logf"], w=["bcum"])
            P.add("dve", lambda e: e.tensor_tensor(out=c3(d1), in0=c3(bcum), in1=c3(bcum)[:, :, 32:33].to_broadcast([128, 8, 64]),
                                                   op=ALU.subtract), r=["bcum"], w=["d1"])
            P.add("act", lambda e: e.activation(out=eqg, in_=d1, func=AF.Exp), r=["d1"], w=["eqg"])
            P.add("act", lambda e: e.activation(out=ekg, in_=d1, func=AF.Exp, scale=-1.0), r=["d1"], w=["ekg"])
            P.add("act", lambda e: e.activation(out=eqd, in_=bcum, func=AF.Exp), r=["bcum"], w=["ff"])
            P.add("dve", lambda e: e.tensor_tensor(out=c3(d1), in0=c3(bcum), in1=c3(bcum)[:, :, 63:64].to_broadcast([128, 8, 64]),
                                                   op=ALU.subtract), r=["bcum", "eqg", "ekg"], w=["d1"])
            P.add("act", lambda e: e.activation(out=ekd, in_=d1, func=AF.Exp, scale=-1.0), r=["d1"], w=["logf"])
            P.add("dve", lambda e, h=h: e.tensor_copy(out=ecd[:, h, :], in_=c3(eqd)[:, :, 63]), r=["ff"], w=["ecd"])
            _mm_fm(self, 2, W["wq"], "wq", h * 128, hT, hk)
            P.add("dve", lambda e, h=h: e.scalar_tensor_tensor(out=qg[:, h, :], in0=self.bank(2), scalar=SC, in1=eqg,
                                                              op0=ALU.mult, op1=ALU.mult), r=[("ps", 2), "eqg"], w=[("qg", h)])
            P.add("dve", lambda e, h=h: e.scalar_tensor_tensor(out=qd[:, h, :], in0=self.bank(2), scalar=SC, in1=eqd,
                                                              op0=ALU.mult, op1=ALU.mult), r=[("ps", 2), "ff"], w=[("qd", h)])
            P.add("dve", lambda e, h=h: e.tensor_tensor(out=kg[:, h, :], in0=kk, in1=ekg, op=ALU.mult), r=["kk", "ekg"],
                  w=[("kg", h)])
            P.add("dve", lambda e: e.tensor_tensor(out=kdf, in0=kk, in1=ekd, op=ALU.mult), r=["kk", "logf"], w=["kdf"])
            pT = self.bank_bf(7)
            for t in range(4):
                P.add("pe", lambda e, t=t: e.transpose(out=pT[:, t * 128:(t + 1) * 128], in_=kdf[:, t * 128:(t + 1) * 128],
                                                        identity=self.idb), r=["kdf", "idb"], w=[("ps", 7)])
            P.add("act", lambda e, h=h: e.copy(out=kdT[:, :, h, :], in_=pT[:, 0:512].rearrange("p (t k) -> p t k", t=4)),
                  r=[("ps", 7)], w=[("kdT", h)])
        if b == 0:
            self.dump("qg", qg, [("qg", h) for h in range(8)])
            self.dump("kg", kg, [("kg", h) for h in range(8)])
            self.dump("qd", qd, [("qd", h) for h in range(8)])
            self.dump("kdT", kdT, [("kdT", h) for h in range(8)])
            self.dump("ecd", ecd, ["ecd"])
        for t in range(4):
            q = t % 2
            for nh in range(2):
                _mm_tm(self, 3 + nh, hT, hk, t, W["wi"], "wi", nh * 512)
                P.add("act", lambda e, t=t, nh=nh: e.copy(out=ivb[:, t, nh * 512:(nh + 1) * 512], in_=self.bank(3 + nh)),
                      r=[("ps", 3 + nh)], w=[("ivb", t)])
            for nh in range(2):
                _mm_tm(self, 3 + nh, hT, hk, t, W["wz"], "wz", nh * 512)
                P.add("act", lambda e, q=q, nh=nh: e.activation(out=sgt[q][:, nh * 512:(nh + 1) * 512], in_=self.bank(3 + nh),
                                                               func=AF.Silu), r=[("ps", 3 + nh)], w=[("sgt", 0)])
            for h in range(8):
                bk = 5 + h // 4
                P.add("pe", lambda e, h=h, t=t, bk=bk: e.matmul(self.bank(bk)[:, (h % 4) * 128:(h % 4 + 1) * 128],
                                                               lhsT=kg[:, h, t * 128:(t + 1) * 128],
                                                               rhs=qg[:, h, t * 128:(t + 1) * 128], start=True, stop=True),
                      r=[("kg", h), ("qg", h)], w=[("ps", bk)])
            for hf in range(2):
                P.add("dve", lambda e, hf=hf, q=q: e.copy_predicated(out=attb[q][:, hf * 512:(hf + 1) * 512],
                                                                    mask=amask.bitcast(U32), data=self.bank(5 + hf)),
                      r=[("ps", 5 + hf), "amask", ("attb", q)], w=[("attb", q)])
            def state_update(c, t=t):
                nonlocal sidx
                rows = slice(c * 64, (c + 1) * 64)
                for h in range(8):
                    bk = 5 + h // 4
                    oc = slice((h % 4) * 128, (h % 4 + 1) * 128)
                    P.add("pe", lambda e, h=h, bk=bk, oc=oc, rows=rows, t=t: e.matmul(
                        self.bank(bk)[:, oc], lhsT=kdT[rows, t, h, :], rhs=ivb[rows, t, h * 128:(h + 1) * 128],
                        start=True, stop=True), r=[("kdT", h), ("ivb", t)], w=[("ps", bk)])
                cc = t * 2 + c
                P.add("dve", lambda e, cc=cc: e.tensor_tensor(out=S.rearrange("p (h v) -> p h v", h=8),
                                                              in0=S.rearrange("p (h v) -> p h v", h=8),
                                                              in1=ecd[:, :, cc:cc + 1].to_broadcast([128, 8, 128]), op=ALU.mult),
                      r=["S", "ecd"], w=["S"])
                for hf in range(2):
                    P.add("dve", lambda e, hf=hf: e.tensor_tensor(out=S[:, hf * 512:(hf + 1) * 512], in0=self.bank(5 + hf),
                                                                  in1=S[:, hf * 512:(hf + 1) * 512], op=ALU.add),
                          r=[("ps", 5 + hf), "S"], w=["S"])
                nxt = 1 - sidx
                P.add("act", lambda e, nxt=nxt: e.copy(out=Sb[nxt], in_=S), r=["S"], w=[("Sb", nxt)])
                sidx = nxt

            s0 = sidx
            state_update(0)
            s1 = sidx
            for h in range(8):
                bk = 1 + h // 4
                oc = slice((h % 4) * 128, (h % 4 + 1) * 128)
                P.add("pe", lambda e, h=h, bk=bk, oc=oc, q=q, t=t: e.matmul(
                    self.bank(bk)[:, oc], lhsT=attb[q][:, h * 128:(h + 1) * 128], rhs=ivb[:, t, h * 128:(h + 1) * 128],
                    start=True, stop=False), r=[("attb", q), ("ivb", t)], w=[("ps", bk)])
                for c, cur in ((0, s0), (1, s1)):
                    rows = slice(c * 64, (c + 1) * 64)
                    cols = slice(t * 128 + c * 64, t * 128 + (c + 1) * 64)
                    P.add("pe", lambda e, h=h, bk=bk, oc=oc, rows=rows, cols=cols, cur=cur, c=c: e.matmul(
                        self.bank(bk)[rows, oc], lhsT=qd[:, h, cols], rhs=Sb[cur][:, h * 128:(h + 1) * 128],
                        start=False, stop=(c == 1)), r=[("qd", h), ("Sb", cur)], w=[("ps", bk)])
            state_update(1)
            for hf in range(2):
                P.add("act", lambda e, hf=hf: e.activation(out=osq[:, hf * 512:(hf + 1) * 512], in_=self.bank(1 + hf),
                                                          func=AF.Square), r=[("ps", 1 + hf)], w=["on"])
            P.add("dve", lambda e: e.tensor_reduce(out=r8, in_=osq.rearrange("p (h v) -> p h v", h=8), axis=AX.X, op=ALU.add),
                  r=["on"], w=["r8"])
            P.add("dve", lambda e: e.tensor_scalar(out=r8, in0=r8, scalar1=1.0 / 128, scalar2=EPS, op0=ALU.mult, op1=ALU.add),
                  r=["r8"], w=["r8"])
            P.add("act", lambda e: e.activation(out=r8, in_=r8, func=AF.Sqrt), r=["r8"], w=["r8"])
            P.add("dve", lambda e: e.reciprocal(out=r8, in_=r8), r=["r8"], w=["r8"])
            for hf in range(2):
                P.add("dve", lambda e, hf=hf: e.tensor_tensor(
                    out=on[:, hf * 512:(hf + 1) * 512].rearrange("p (h v) -> p h v", h=4),
                    in0=self.bank(1 + hf).rearrange("p (h v) -> p h v", h=4),
                    in1=r8[:, hf * 4:(hf + 1) * 4].unsqueeze(2).to_broadcast([128, 4, 128]), op=ALU.mult),
                    r=[("ps", 1 + hf), "r8"], w=["on"])
            P.add("dve", lambda e: e.tensor_tensor(out=on, in0=on, in1=hg, op=ALU.mult), r=["on", "hg"], w=["on"])
            P.add("dve", lambda e, q=q: e.tensor_tensor(out=yb, in0=on, in1=sgt[q], op=ALU.mult), r=["on", ("sgt", 0)], w=["yb"])
            self.transpose_tile(yb, "yb", mixT, MK, t * 128, 0)
        if b == 0:
            self.dump("on", on, ["on"])
            self.dump("S", S, ["S"])
            self.dump("mixT", mixT, [MK])
        _out_proj(self, b, st, mixT, MK, wo, xo)


Builder.hgrn = _hgrn


def _ssd(self, L, xn, xa, xo):
    P, A = self.P, self.A
    e_ = L // 2
    P.barrier()
    A.reset(self.base_mark)
    wz = A.bf16(8 * 1024).rearrange("p (k n) -> p k n", k=8)
    wx = A.bf16(8 * 1536).rearrange("p (k n) -> p k n", k=8)
    wdt = A.bf16(8 * 16).rearrange("p (k n) -> p k n", k=8)
    wo = A.bf16(8 * 1024).rearrange("p (k n) -> p k n", k=8)
    self.load_w(wx, _w_in_view(self, "even_w_in", e_, 1024, 1536), "wx", 4)
    self.load_w(wdt, _w_in_view(self, "even_w_in", e_, 2560, 16), "wdt", 1)
    self.load_w(wz, _w_in_view(self, "even_w_in", e_, 0, 1024), "wz", 4)
    self.load_w(wo, self.dram["even_w_out"][e_][0:1024, :].rearrange("(k p) n -> p k n", p=128), "wo", 4)
    gt = A.f32(1024)
    ng = A.f32(1024)
    self.load_bcast(gt, self.dram["mix_norm_g"][L:L + 1, :], "gt")
    self.load_bcast(ng, self.dram["ssd_norm_g"][e_:e_ + 1, :], "ng")
    dtb = A.f32(16)
    aneg = A.f32(16)
    dsk = A.f32(16)
    self.load_bcast(dtb, self.dram["ssd_dt_bias"][e_:e_ + 1, :], "dtb")
    self.load_bcast(aneg, self.dram["ssd_a_log"][e_:e_ + 1, :], "aneg")
    self.load_bcast(dsk, self.dram["ssd_d"][e_:e_ + 1, :], "dsk")
    P.add("act", lambda e: e.activation(out=aneg, in_=aneg, func=AF.Exp), r=["aneg"], w=["aneg"])
    P.add("dve", lambda e: e.tensor_scalar(out=aneg, in0=aneg, scalar1=-1.0, scalar2=0.0, op0=ALU.mult, op1=ALU.add),
          r=["aneg"], w=["aneg"])
    maskT = A.f32(128)
    maskTb = A.bf16(128)
    strict = A.f32(128)
    onesf = A.f32(128)
    P.add("sp", lambda e: e.dma_start(out=maskT, in_=self.dram["c_maskT"]), w=["maskT"], dma="maskT")
    P.add("sp", lambda e: e.dma_start(out=strict, in_=self.dram["c_strict"]), w=["strict"], dma="strict")
    P.add("dve", lambda e: e.tensor_copy(out=maskTb, in_=maskT), r=["maskT"], w=["maskTb"])
    P.add("dve", lambda e: e.memset(onesf, 1.0), w=["onesf"])
    DI = A.bf16(16 * 128).rearrange("p (h l) -> p h l", h=16)
    for h in range(16):
        P.add("dve", lambda e, h=h: e.scalar_tensor_tensor(out=DI[:, h, :], in0=self.idf, scalar=dsk[:, h:h + 1], in1=self.idf,
                                                          op0=ALU.mult, op1=ALU.mult), r=["idf", "dsk"], w=["DI"])
    cw = A.f32(12 * 4).rearrange("p (m j) -> p m j", m=12)
    cb = A.f32(12).rearrange("p (m j) -> p m j", m=12)
    halo = A.f32(36).rearrange("p (m j) -> p m j", m=12)
    mk = A.mark()
    _load_cols(self, cw, self.dram["ssd_conv_w"][e_], 4, "cw", nch=12, bk=1)
    _load_cols(self, cb, self.dram["ssd_conv_b"][e_:e_ + 1, :], 1, "cb", nch=12, bk=2)
    P.barrier(keep=("wz", "wx", "wdt", "wo"))
    A.reset(mk)
    st = _alloc_blk(self, xa is xn)
    xr = [A.f32(TB + 3) for i in range(2)]
    acc = [A.f32(TB) for i in range(2)]
    xsb = [A.bf16(TB) for i in range(2)]
    BC = A.bf16(4 * TB).rearrange("p (m n) -> p m n", m=4)
    xs_tm = A.bf16(4 * 1024).rearrange("p (t n) -> p t n", t=4)
    Btm = A.bf16(4 * 2 * 128).rearrange("p (t g n) -> p t g n", t=4, g=2)
    zsl = [A.f32(1024) for i in range(2)]
    sml = [A.f32(16 * 10).rearrange("p (j h) -> p j h", j=10) for i in range(2)]
    wbl = [A.bf16(16) for i in range(2)]
    rsegl = [A.f32(16 * 128).rearrange("p (h l) -> p h l", h=16) for i in range(2)]
    attl = [A.bf16(16 * 128).rearrange("p (h l) -> p h l", h=16) for i in range(2)]
    cbml = [A.bf16(256).rearrange("p (g l) -> p g l", g=2) for i in range(2)]
    xwl = [A.bf16(1024) for i in range(2)]
    yl = [A.f32(1024) for i in range(2)]
    ss2l = [A.f32(4) for i in range(2)]
    state = A.f32(1024)
    stb = [A.bf16(1024) for i in range(2)]
    ybl = [A.bf16(1024) for i in range(2)]
    mixT = st["hT"][0]
    MK = ("hT", 0)
    nblk_seq = self.seq // TB
    sidx = 0
    for b in range(self.ntok // TB):
        _prologue(self, b, st, xn, xa, gt)
        hT = st["hT"][0]
        hk = ("hT", 0)
        if b % nblk_seq == 0:
            P.add("pool", lambda e: e.memset(state, 0.0), w=["state"])
            P.add("pool", lambda e, sidx=sidx: e.memset(stb[sidx], 0.0), w=[("stb", sidx)])
            P.add("pool", lambda e: e.memset(halo, 0.0), w=["halo"])
        for m in range(12):
            q = m % 2
            bk = 1 + q
            _mm_fm(self, bk, wx, "wx", m * 128, hT, hk)
            P.add("act", lambda e, q=q, bk=bk: e.copy(out=xr[q][:, 3:3 + TB], in_=self.bank(bk)), r=[("ps", bk)], w=[("xr", q)])
            P.add("pool", lambda e, q=q, m=m: e.tensor_copy(out=xr[q][:, 0:3], in_=halo[:, m, :]), r=["halo"], w=[("xr", q)])
            P.add("act", lambda e, q=q, m=m: e.activation(out=acc[q], in_=xr[q][:, 3:3 + TB], func=AF.Identity,
                                                         scale=cw[:, m, 3:4], bias=cb[:, m, 0:1]), r=[("xr", q), "cw", "cb"],
                  w=[("acc", q)])
            for j in range(3):
                P.add("dve", lambda e, q=q, m=m, j=j: e.scalar_tensor_tensor(out=acc[q], in0=xr[q][:, j:j + TB], scalar=cw[:, m, j:j + 1],
                                                                            in1=acc[q], op0=ALU.mult, op1=ALU.add),
                      r=[("xr", q), ("acc", q), "cw"], w=[("acc", q)])
            P.add("pool", lambda e, q=q, m=m: e.tensor_copy(out=halo[:, m, :], in_=xr[q][:, TB:TB + 3]), r=[("xr", q)], w=["halo"])
            if m < 8:
                P.add("act", lambda e, q=q: e.activation(out=xsb[q], in_=acc[q], func=AF.Silu), r=[("acc", q)], w=[("xsb", q)])
                src = xsb[q]
                skey = ("xsb", q)
            else:
                P.add("act", lambda e, q=q, m=m: e.activation(out=BC[:, m - 8, :], in_=acc[q], func=AF.Silu), r=[("acc", q)],
                      w=[("BC", m - 8)])
                src = BC[:, m - 8, :]
                skey = ("BC", m - 8)
            if m < 10:
                pT = self.bank_bf(0)
                for t in range(4):
                    P.add("pe", lambda e, t=t, src=src: e.transpose(out=pT[:, t * 128:(t + 1) * 128], in_=src[:, t * 128:(t + 1) * 128],
                                                                    identity=self.idb), r=[skey, "idb"], w=[("ps", 0)])
                if m < 8:
                    P.add("act", lambda e, m=m: e.copy(out=xs_tm[:, :, m * 128:(m + 1) * 128],
                                                       in_=pT[:, 0:512].rearrange("p (t c) -> p t c", t=4)),
                          r=[("ps", 0)], w=[("xs_tm", m)])
                else:
                    P.add("act", lambda e, m=m: e.copy(out=Btm[:, :, m - 8, :], in_=pT[:, 0:512].rearrange("p (t c) -> p t c", t=4)),
                          r=[("ps", 0)], w=["Btm"])
        xsk = [("xs_tm", m) for m in range(8)]
        def tile(t):
            nonlocal sidx
            q = t % 2
            zs, sm, wb, rseg, att, cbm, xw, y, ss2, yb = (zsl[q], sml[q], wbl[q], rsegl[q], attl[q], cbml[q], xwl[q], yl[q],
                                                          ss2l[q], ybl[q])
            dt, lndt, dA, acs, tot, dte, dfs, cdec, w32 = [sm[:, j, :] for j in range(9)]
            tc_ = slice(t * 128, (t + 1) * 128)
            for nh in range(2):
                _mm_tm(self, 5 + nh, hT, hk, t, wz, "wz", nh * 512)
                P.add("act", lambda e, nh=nh: e.activation(out=zs[:, nh * 512:(nh + 1) * 512], in_=self.bank(5 + nh), func=AF.Silu),
                      r=[("ps", 5 + nh)], w=[("zs", q)])
            b7 = self.bank(7)
            for k in range(8):
                P.add("pe", lambda e, k=k, t=t: e.matmul(b7[:, 256:272], lhsT=hT[:, k, t * 128:(t + 1) * 128], rhs=wdt[:, k, :],
                                                        start=(k == 0), stop=(k == 7)), r=[hk, ("wdt", 0, 0)], w=[("ps", 7)])
            P.add("dve", lambda e: e.tensor_tensor(out=dt, in0=b7[:, 256:272], in1=dtb, op=ALU.add), r=[("ps", 7), "dtb"], w=[("sm", q)])
            P.add("act", lambda e: e.activation(out=dt, in_=dt, func=AF.Exp), r=[("sm", q)], w=[("sm", q)])
            P.add("act", lambda e: e.activation(out=dt, in_=dt, func=AF.Ln, bias=1.0, scale=1.0), r=[("sm", q)], w=[("sm", q)])
            P.add("act", lambda e: e.activation(out=lndt, in_=dt, func=AF.Ln), r=[("sm", q)], w=[("sm", q)])
            P.add("dve", lambda e: e.tensor_tensor(out=dA, in0=dt, in1=aneg, op=ALU.mult), r=[("sm", q), "aneg"], w=[("sm", q)])
            P.add("pe", lambda e: e.matmul(b7[:, 272:288], lhsT=maskT, rhs=dA, start=True, stop=True), r=["maskT", ("sm", q)], w=[("ps", 7)])
            P.add("pe", lambda e: e.matmul(b7[:, 288:304], lhsT=onesf, rhs=dA, start=True, stop=True), r=["onesf", ("sm", q)], w=[("ps", 7)])
            P.add("act", lambda e: e.copy(out=acs, in_=b7[:, 272:288]), r=[("ps", 7)], w=[("sm", q)])
            P.add("act", lambda e: e.copy(out=tot, in_=b7[:, 288:304]), r=[("ps", 7)], w=[("sm", q)])
            P.add("dve", lambda e: e.tensor_tensor(out=dte, in0=tot, in1=acs, op=ALU.subtract), r=[("sm", q)], w=[("sm", q)])
            P.add("act", lambda e: e.activation(out=dte, in_=dte, func=AF.Exp), r=[("sm", q)], w=[("sm", q)])
            P.add("act", lambda e: e.activation(out=dfs, in_=acs, func=AF.Exp), r=[("sm", q)], w=[("sm", q)])
            P.add("act", lambda e: e.activation(out=cdec, in_=tot, func=AF.Exp), r=[("sm", q)], w=[("sm", q)])
            P.add("dve", lambda e: e.tensor_tensor(out=wb, in0=dte, in1=dt, op=ALU.mult), r=[("sm", q)], w=[("wb", q)])
            P.add("dve", lambda e: e.tensor_tensor(out=rseg, in0=maskT.unsqueeze(1).to_broadcast([128, 16, 128]),
                                                   in1=dA.unsqueeze(2).to_broadcast([128, 16, 128]), op=ALU.mult),
                  r=["maskT", ("sm", q)], w=[("rseg", q)])
            for j in range(4):
                sb_ = 1 + j % 2
                P.add("pe", lambda e, j=j, sb_=sb_: e.matmul(self.bank(sb_), lhsT=strict, rhs=rseg[:, j * 4:(j + 1) * 4, :].rearrange("p h l -> p (h l)"),
                                                            start=True, stop=True), r=["strict", ("rseg", q)], w=[("ps", sb_)])
                for h in range(j * 4, j * 4 + 4):
                    P.add("act", lambda e, h=h, sb_=sb_: e.activation(out=att[:, h, :], in_=self.bank(sb_)[:, (h % 4) * 128:(h % 4 + 1) * 128],
                                                                     func=AF.Exp, bias=lndt[:, h:h + 1], scale=1.0),
                          r=[("ps", sb_), ("sm", q)], w=[("att", q)])
            for g in range(2):
                P.add("pe", lambda e, g=g, tc_=tc_: e.matmul(b7[:, g * 128:(g + 1) * 128], lhsT=BC[:, g, tc_], rhs=BC[:, 2 + g, tc_],
                                                            start=True, stop=True), r=[("BC", g), ("BC", 2 + g)], w=[("ps", 7)])
            P.add("act", lambda e: e.copy(out=cbm.rearrange("p g l -> p (g l)"), in_=b7[:, 0:256]), r=[("ps", 7)], w=[("cbm", q)])
            for g in range(2):
                P.add("dve", lambda e, g=g: e.tensor_tensor(out=cbm[:, g, :], in0=cbm[:, g, :], in1=maskTb, op=ALU.mult),
                      r=[("cbm", q), "maskTb"], w=[("cbm", q)])
            for g in range(2):
                P.add("dve", lambda e, g=g: e.tensor_tensor(out=att[:, g * 8:(g + 1) * 8, :], in0=att[:, g * 8:(g + 1) * 8, :],
                                                            in1=cbm[:, g, :].unsqueeze(1).to_broadcast([128, 8, 128]), op=ALU.mult),
                      r=[("att", q), ("cbm", q)], w=[("att", q)])
            P.add("dve", lambda e: e.tensor_tensor(out=att, in0=att, in1=DI, op=ALU.add), r=[("att", q), "DI"], w=[("att", q)])
            cur = sidx
            for h in range(16):
                bk = 5 + h // 8
                P.add("pe", lambda e, h=h, bk=bk, t=t: e.matmul(self.bank(bk)[:, (h % 8) * 64:(h % 8 + 1) * 64], lhsT=att[:, h, :],
                                                               rhs=xs_tm[:, t, h * 64:(h + 1) * 64], start=True, stop=True),
                      r=[("att", q)] + xsk, w=[("ps", bk)])
            for g in range(2):
                P.add("pe", lambda e, g=g, tc_=tc_, cur=cur: e.matmul(self.bank(3 + g), lhsT=BC[:, 2 + g, tc_],
                                                                     rhs=stb[cur][:, g * 512:(g + 1) * 512], start=True, stop=True),
                      r=[("BC", 2 + g), ("stb", cur)], w=[("ps", 3 + g)])
            for g in range(2):
                P.add("dve", lambda e, g=g: e.tensor_tensor(out=y[:, g * 512:(g + 1) * 512].rearrange("p (h v) -> p h v", h=8),
                                                            in0=self.bank(3 + g).rearrange("p (h v) -> p h v", h=8),
                                                            in1=dfs[:, g * 8:(g + 1) * 8].unsqueeze(2).to_broadcast([128, 8, 64]),
                                                            op=ALU.mult), r=[("ps", 3 + g), ("sm", q)], w=[("y", q)])
                P.add("dve", lambda e, g=g: e.tensor_tensor(out=y[:, g * 512:(g + 1) * 512], in0=self.bank(5 + g),
                                                            in1=y[:, g * 512:(g + 1) * 512], op=ALU.add), r=[("ps", 5 + g), ("y", q)], w=[("y", q)])
            P.add("dve", lambda e, t=t: e.tensor_tensor(out=xw.rearrange("p (h v) -> p h v", h=16),
                                                        in0=xs_tm[:, t, :].rearrange("p (h v) -> p h v", h=16),
                                                        in1=wb.unsqueeze(2).to_broadcast([128, 16, 64]), op=ALU.mult),
                  r=xsk + [("wb", q)], w=[("xw", q)])
            for g in range(2):
                P.add("pe", lambda e, g=g, t=t: e.matmul(self.bank(3 + g), lhsT=Btm[:, t, g, :], rhs=xw[:, g * 512:(g + 1) * 512],
                                                        start=True, stop=True), r=["Btm", ("xw", q)], w=[("ps", 3 + g)])
            P.add("dve", lambda e: e.tensor_tensor(out=state.rearrange("p (h v) -> p h v", h=16),
                                                   in0=state.rearrange("p (h v) -> p h v", h=16),
                                                   in1=cdec.unsqueeze(2).to_broadcast([128, 16, 64]), op=ALU.mult),
                  r=["state", ("sm", q)], w=["state"])
            for g in range(2):
                P.add("dve", lambda e, g=g: e.tensor_tensor(out=state[:, g * 512:(g + 1) * 512], in0=self.bank(3 + g),
                                                            in1=state[:, g * 512:(g + 1) * 512], op=ALU.add),
                      r=[("ps", 3 + g), "state"], w=["state"])
            nxt = 1 - sidx
            P.add("act", lambda e, nxt=nxt: e.copy(out=stb[nxt], in_=state), r=["state"], w=[("stb", nxt)])
            sidx = nxt
            P.add("dve", lambda e: e.tensor_tensor(out=y, in0=y, in1=zs, op=ALU.mult), r=[("y", q), ("zs", q)], w=[("y", q)])
            jk = st["scr"][0]["junk"]
            jkk = st["scr"][0]["jkey"]
            for g in range(2):
                P.add("act", lambda e, g=g: e.activation(out=jk[:, g * 512:(g + 1) * 512], in_=y[:, g * 512:(g + 1) * 512],
                                                        func=AF.Square, accum_out=ss2[:, g:g + 1]), r=[("y", q)], w=[jkk, ("ss2", q)])
            P.add("dve", lambda e: e.tensor_scalar(out=ss2[:, 0:2], in0=ss2[:, 0:2], scalar1=1.0 / 512, scalar2=EPS, op0=ALU.mult,
                                                   op1=ALU.add), r=[("ss2", q)], w=[("ss2", q)])
            P.add("act", lambda e: e.activation(out=ss2[:, 0:2], in_=ss2[:, 0:2], func=AF.Sqrt), r=[("ss2", q)], w=[("ss2", q)])
            P.add("dve", lambda e: e.reciprocal(out=ss2[:, 0:2], in_=ss2[:, 0:2]), r=[("ss2", q)], w=[("ss2", q)])
            for g in range(2):
                P.add("dve", lambda e, g=g: e.scalar_tensor_tensor(out=yb[:, g * 512:(g + 1) * 512], in0=y[:, g * 512:(g + 1) * 512],
                                                                  scalar=ss2[:, g:g + 1], in1=ng[:, g * 512:(g + 1) * 512],
                                                                  op0=ALU.mult, op1=ALU.mult), r=[("y", q), ("ss2", q), "ng"], w=[("yb", q)])
            self.transpose_tile(yb, ("yb", q), mixT, MK, t * 128, 0)
            if b == 0 and t == 1:
                self.dump(("sm", q), sm, [("sm", q)])
                self.dump(("att", q), att, [("att", q)])
                self.dump(("y", q), y, [("y", q)])
                self.dump("state", state, ["state"])
                self.dump("xs_tm", xs_tm, xsk)
                self.dump("BC", BC, [("BC", i) for i in range(4)])
        for t in range(4):
            tile(t)
        _out_proj(self, b, st, mixT, MK, wo, xo)


Builder.ssd = _ssd


def _ffn_full(self, L, xn, xo, final_out=None):
    P, A = self.P, self.A
    P.barrier()
    A.reset(self.base_mark)
    NM = DFF // 128
    w1 = A.bf16(8 * DFF).rearrange("p (k n) -> p k n", k=8)
    w2 = A.bf16(NM * 1024).rearrange("p (k n) -> p k n", k=NM)
    gt = A.f32(1024)
    self.load_w(w1, self.dram["ffn_w1"][L].rearrange("(k p) n -> p k n", p=128), "w1", 8)
    self.load_w(w2, self.dram["ffn_w2"][L].rearrange("(k p) n -> p k n", p=128), "w2", 4)
    self.load_bcast(gt, self.dram["ffn_norm_g"][L:L + 1, :], "gt")
    xt = [A.f32(1024) for i in range(2)]
    xa = [A.f32(1024) for i in range(2)]
    if final_out is None:
        hb = [A.bf16(1024) for i in range(2)]
    else:
        hb1 = A.bf16(1024)
        hb = [hb1, hb1]
    nhb = 2 if final_out is None else 1
    hT = [A.bf16(8 * TB).rearrange("p (k n) -> p k n", k=8) for s in range(2)]
    jk = A.bf16(1024)
    scr = [dict(junk=jk, jkey="junk", ss=A.f32(1), rs=A.f32(1), key="scr%d" % s) for s in range(2)]
    if final_out is None:
        rr = [A.f32(512) for i in range(2)]
    else:
        rr1 = A.f32(512)
        rr = [rr1, rr1]
        gf = A.f32(1024)
        self.load_bcast(gf, self.dram["final_norm_g"].rearrange("(o n) -> o n", o=1), "gf")
        fscr = [dict(junk=jk, jkey="junk", ss=A.f32(1), rs=A.f32(1), key="fscr%d" % s) for s in range(2)]
    nrr = 2 if final_out is None else 1
    uT = A.bf16(NM * TB).rearrange("p (k n) -> p k n", k=NM)
    nblk = self.ntok // TB

    def prologue(b):
        s = b % 2
        for t in range(4):
            i = t % 2
            row = b * TB + t * 128
            P.add("sp", lambda e, i=i, row=row: e.dma_start(out=xt[i], in_=xn[row:row + 128, :]), w=[("xt", i)], dma=("xt", i))
            self.norm_tile(xt[i], ("xt", i), gt, "gt", hb[i], ("hb", i % nhb), scr[i])
            self.transpose_tile(hb[i], ("hb", i % nhb), hT[s], ("hT", s), t * 128, 0)

    def ffn1(b):
        s = b % 2
        for m in range(NM):
            bk = 1 + m % 2
            for k in range(8):
                P.add("pe", lambda e, m=m, k=k, bk=bk: e.matmul(self.bank(bk), lhsT=w1[:, k, m * 128:(m + 1) * 128],
                                                               rhs=hT[s][:, k, :], start=(k == 0), stop=(k == 7)),
                      r=self.wk("w1", m * 128, 128) + [("hT", s)], w=[("ps", bk)])
            q = m % nrr
            P.add("act", lambda e, bk=bk, q=q: e.activation(out=rr[q], in_=self.bank(bk), func=AF.Relu),
                  r=[("ps", bk)], w=[("rr", q)])
            P.add("dve", lambda e, m=m, q=q: e.tensor_tensor(out=uT[:, m, :], in0=rr[q], in1=rr[q], op=ALU.mult),
                  r=[("rr", q)], w=[("uT", m)])

    def ffn2(b):
        for t in range(4):
            i = t % 2
            row = b * TB + t * 128
            P.add("sp", lambda e, i=i, row=row: e.dma_start(out=xa[i], in_=xn[row:row + 128, :]), w=[("xa", i)], dma=("xa", i))
            for nh in range(2):
                bk = 3 + nh
                for m in range(NM):
                    P.add("pe", lambda e, m=m, t=t, nh=nh, bk=bk: e.matmul(
                        self.bank(bk), lhsT=uT[:, m, t * 128:(t + 1) * 128], rhs=w2[:, m, nh * 512:(nh + 1) * 512],
                        start=(m == 0), stop=(m == NM - 1)), r=[("uT", m)] + self.wk("w2", nh * 512, 512), w=[("ps", bk)])
                P.add("dve", lambda e, i=i, nh=nh, bk=bk: e.tensor_tensor(
                    out=xa[i][:, nh * 512:(nh + 1) * 512], in0=self.bank(bk), in1=xa[i][:, nh * 512:(nh + 1) * 512],
                    op=ALU.add), r=[("ps", bk), ("xa", i)], w=[("xa", i)])
            if final_out is None:
                P.add("sp", lambda e, i=i, row=row: e.dma_start(out=xo[row:row + 128, :], in_=xa[i]),
                      r=[("xa", i)], w=[("dram", id(xo), row)], dma=("xst", i))
            else:
                self.norm_tile(xa[i], ("xa", i), gf, "gf", xa[i], ("xa", i), fscr[i])
                P.add("sp", lambda e, i=i, row=row: e.dma_start(out=final_out[row:row + 128, :], in_=xa[i]),
                      r=[("xa", i)], w=[("dram", "out", row)], dma=("xst", i))

    prologue(0)
    for b in range(nblk):
        ffn1(b)
        if b + 1 < nblk:
            prologue(b + 1)
        ffn2(b)


Builder.ffn_full = _ffn_full
```

```python
import numpy as np
from contextlib import ExitStack
import concourse.bass as bass
import concourse.mybir as mybir
import concourse.bass_utils as bu

F32 = mybir.dt.float32
BF16 = mybir.dt.bfloat16
AF = mybir.ActivationFunctionType
ALU = mybir.AluOpType
AX = mybir.AxisListType

D = 1024
DFF = 4096
EPS = 1e-5
ENGS = ["pe", "act", "dve", "pool", "sp"]


class Op:
    __slots__ = ("eng", "fn", "deps", "odeps", "dma", "dma_val", "needs_inc", "inc_idx", "cost", "lat", "keep", "dkey", "grp")


class _Probe:
    def __getattr__(self, name):
        def f(*a, **kw):
            self.rec = (name, a, kw)
            return None
        return f


def _act_group(fn):
    pr = _Probe()
    pr.rec = None
    try:
        fn(pr)
    except Exception:
        return None
    if pr.rec is None:
        return None
    name, a, kw = pr.rec
    if name != "activation":
        return None
    f = kw.get("func", None)
    if f in (AF.Copy, AF.Identity, AF.Relu):
        return None
    if f in (AF.Exp, AF.Ln):
        return "explog"
    return str(f)


def _op_cost(eng, fn, dma):
    pr = _Probe()
    pr.rec = None
    try:
        fn(pr)
    except Exception:
        pr.rec = None
    if pr.rec is None:
        return 500.0, 500.0
    name, a, kw = pr.rec
    out = kw.get("out", a[0] if a else None)
    try:
        shp = list(out.shape)
        n = 1
        for d in shp[1:]:
            n *= int(d)
        npart = int(shp[0])
    except Exception:
        n, npart = 512, 128
    if dma is not None:
        nbytes = n * npart * (2 if out.dtype == BF16 else 4)
        issue = 1500.0 if eng == "pool" else 120.0
        return issue, 2500.0 + nbytes / 150.0
    if eng == "pe":
        c = max(n, 64) / 2.4 + 12.0
        lhsT = kw.get("lhsT", None)
        if lhsT is not None and lhsT.dtype == F32:
            c *= 4.0
        return c, c + 160.0
    if eng == "act":
        c = 200.0 + n * 0.75
    elif eng == "dve":
        c = 90.0 + n * (1.05 if out.dtype == F32 or name in ("tensor_tensor_scan",) else 0.8)
    else:
        c = 250.0 + n * 2.0
    return c, c


class Prog:
    def __init__(self):
        self.ops = []
        self.lastw = {}
        self.rds = {}
        self.dma_cnt = {}
        self.dma_slot = {}
        self.free_slots_e = {}
        self.slot_eng = {}
        self.nslots = 0

    def _slot(self, key, eng):
        if key not in self.dma_slot:
            fl = self.free_slots_e.setdefault(eng, [])
            if fl:
                sl = fl.pop()
            else:
                sl = self.nslots
                self.nslots += 1
                self.slot_eng[sl] = eng
            self.dma_slot[key] = sl
        return self.dma_slot[key]

    def add(self, eng, fn, r=(), w=(), dma=None):
        op = Op()
        op.eng = eng
        op.fn = fn
        op.dkey = dma
        if dma is not None:
            dma = self._slot(dma, eng)
        op.dma = dma
        op.needs_inc = False
        op.inc_idx = 0
        op.dma_val = 0
        deps = {}
        for k in r:
            lw = self.lastw.get(k)
            if lw is not None:
                deps[lw] = "raw"
        for k in w:
            lw = self.lastw.get(k)
            if lw is not None and lw not in deps:
                deps[lw] = "waw"
            for rd in self.rds.get(k, ()):
                if rd not in deps:
                    deps[rd] = "war"
        i = len(self.ops)
        fd = []
        for a, typ in deps.items():
            A = self.ops[a]
            if A.dma is None:
                if A.eng == eng and typ != "raw" and dma is None:
                    continue
                A.needs_inc = True
            fd.append(a)
        op.deps = fd
        op.odeps = list(deps.keys())
        op.cost, op.lat = _op_cost(eng, fn, dma)
        op.grp = _act_group(fn) if (eng == "act" and dma is None) else None
        if dma is not None:
            self.dma_cnt[dma] = self.dma_cnt.get(dma, 0) + 16
            op.dma_val = self.dma_cnt[dma]
        self.ops.append(op)
        for k in r:
            self.rds.setdefault(k, []).append(i)
        for k in w:
            self.lastw[k] = i
            self.rds[k] = []
        return i

    def barrier(self, keep=()):
        keep = set(keep)
        for eng in ENGS:
            op = Op()
            op.eng = eng
            op.fn = None
            op.dma = None
            op.needs_inc = False
            op.inc_idx = 0
            op.dma_val = 0
            op.deps = []
            op.odeps = []
            op.cost = op.lat = 0.0
            op.keep = keep
            op.dkey = None
            op.grp = None
            self.ops.append(op)
        iskept = lambda k: isinstance(k, tuple) and k[0] in keep
        for k in list(self.dma_slot):
            if not iskept(k):
                sl = self.dma_slot.pop(k)
                self.free_slots_e.setdefault(self.slot_eng[sl], []).append(sl)
        self.lastw = {k: v for k, v in self.lastw.items() if iskept(k)}
        self.rds = {k: [] for k in self.lastw}

    def _fix_barriers(self):
        last = {}
        for i, op in enumerate(self.ops):
            if op.fn is None:
                op.deps = [a for k, a in last.items() if not (k[0] == "e" and k[1] == op.eng)
                           and not (k[0] == "d" and isinstance(self.ops[a].dkey, tuple) and self.ops[a].dkey[0] in op.keep)]
                for a in op.deps:
                    if self.ops[a].dma is None:
                        self.ops[a].needs_inc = True
                continue
            if op.dma is not None:
                k = ("d", op.dma)
                if k not in last or self.ops[last[k]].dma_val < op.dma_val:
                    last[k] = i
            else:
                last[("e", op.eng)] = i

    def schedule(self, xlat=350.0):
        import heapq
        ops = self.ops
        n = len(ops)
        new_order = []
        i = 0
        while i < n:
            if ops[i].fn is None:
                new_order.append(i)
                i += 1
                continue
            j = i
            while j < n and ops[j].fn is not None:
                j += 1
            seg = range(i, j)
            indeg = {}
            users = {}
            for k in seg:
                cnt = 0
                for a in ops[k].odeps:
                    if a >= i:
                        cnt += 1
                        users.setdefault(a, []).append(k)
                indeg[k] = cnt
            blev = {}
            for k in reversed(seg):
                m_ = 0.0
                for u in users.get(k, ()):
                    v = blev[u] + (xlat if ops[u].eng != ops[k].eng else 0.0)
                    if v > m_:
                        m_ = v
                blev[k] = ops[k].lat + m_
            fin = {}
            last_grp = None
            efree = {e: 0.0 for e in ENGS}
            ready = {e: [] for e in ENGS}
            rtime = {}
            for k in seg:
                if indeg[k] == 0:
                    rtime[k] = 0.0
                    heapq.heappush(ready[ops[k].eng], (0.0, k))
            done = 0
            total = j - i
            while done < total:
                best = None
                for e in ENGS:
                    h = ready[e]
                    if not h:
                        continue
                    ef = efree[e]
                    rt, k = h[0]
                    if rt <= ef:
                        cand = []
                        slack_ = 500.0 if (e == "act" and ACT_TABLE_AWARE) else (250.0 if e in ("dve", "pe") else 0.0)
                        while h and h[0][0] <= ef + slack_:
                            cand.append(heapq.heappop(h))
                        if e == "act" and ACT_TABLE_AWARE:
                            lg_ = last_grp
                            kk = max(cand, key=lambda c: (1 if (ops[c[1]].grp is None or ops[c[1]].grp == lg_) else 0,
                                                          blev[c[1]], -c[1]))[1]
                        else:
                            kk = max(cand, key=lambda c: (blev[c[1]], -c[1]))[1] if PRIO_CP else min(c[1] for c in cand)
                        for c in cand:
                            if c[1] != kk:
                                heapq.heappush(h, (c[0], c[1]))
                        st_ = max(ef, rtime[kk])
                        k = kk
                        popped = True
                    else:
                        st_ = rt
                        popped = False
                    if best is None or st_ < best[0] or (st_ == best[0] and k < best[1]):
                        if best is not None and best[3]:
                            heapq.heappush(ready[ops[best[1]].eng], (rtime[best[1]], best[1]))
                        best = (st_, k, e, popped)
                    elif popped:
                        heapq.heappush(h, (rtime[k], k))
                st_, k, e, popped = best
                if not popped:
                    heapq.heappop(ready[e])
                op = ops[k]
                sw_ = 0.0
                if e == "act" and op.grp is not None:
                    if op.grp != last_grp:
                        sw_ = 1300.0
                    last_grp = op.grp
                efree[e] = st_ + op.cost + sw_
                fin[k] = st_ + op.lat + sw_
                new_order.append(k)
                done += 1
                for u in users.get(k, ()):
                    indeg[u] -= 1
                    t_ = fin[k] + (xlat if (ops[u].eng != e or op.dma is not None) else 120.0)
                    if t_ > rtime.get(u, 0.0):
                        rtime[u] = t_
                    if indeg[u] == 0:
                        heapq.heappush(ready[ops[u].eng], (rtime[u], u))
            i = j
        remap = {old: new for new, old in enumerate(new_order)}
        nops = [ops[k] for k in new_order]
        for op in nops:
            op.deps = [remap[a] for a in op.deps]
            op.odeps = [remap[a] for a in op.odeps]
        self.ops = nops

    def emit(self, nc):
        self._fix_barriers()
        cnt = {e: 0 for e in ENGS}
        by_eng = {e: [] for e in ENGS}
        for op in self.ops:
            if op.dma is None and op.needs_inc:
                cnt[op.eng] += 1
                op.inc_idx = cnt[op.eng]
            by_eng[op.eng].append(op)
        with ExitStack() as es:
            esem = {e: es.enter_context(nc.semaphore("s_" + e)) for e in ENGS}
            dsem = {}
            for n, k in enumerate(self.dma_cnt):
                dsem[k] = es.enter_context(nc.semaphore("d%d" % n))
            block = es.enter_context(nc.Block())
            ops = self.ops
            dma_cnt = self.dma_cnt

            def body(e, eng):
                waited = {}
                for op in by_eng[eng]:
                    need = {}
                    for a in op.deps:
                        A = ops[a]
                        if A.dma is not None:
                            s, v = dsem[A.dma], A.dma_val
                        else:
                            s, v = esem[A.eng], A.inc_idx
                        if need.get(s, (None, 0))[1] < v:
                            need[s] = (s, v)
                    for s, v in need.values():
                        if waited.get(s, 0) < v:
                            e.wait_ge(s, v)
                            waited[s] = v
                    if op.fn is None:
                        continue
                    ins = op.fn(e)
                    if op.dma is not None:
                        ins.then_inc(dsem[op.dma], 16)
                    elif op.needs_inc:
                        ins.then_inc(esem[eng], 1)
                if eng == "sp":
                    for k, s in dsem.items():
                        if waited.get(s, 0) < dma_cnt[k]:
                            e.wait_ge(s, dma_cnt[k])

            block.tensor(lambda e: body(e, "pe"))
            block.scalar(lambda e: body(e, "act"))
            block.vector(lambda e: body(e, "dve"))
            block.gpsimd(lambda e: body(e, "pool"))
            block.sync(lambda e: body(e, "sp"))


class Arena:
    def __init__(self, ap, nwords):
        self.ap = ap
        self.n = nwords
        self.off = 0

    def mark(self):
        return self.off

    def reset(self, m):
        self.off = m

    def f32(self, n):
        a = self.off
        self.off += (n + 7) // 8 * 8
        assert self.off <= self.n, ("SBUF arena overflow", self.off, self.n)
        return self.ap[:, a:a + n]

    def bf16(self, n):
        w = (n + 1) // 2
        a = self.off
        self.off += (w + 7) // 8 * 8
        assert self.off <= self.n, ("SBUF arena overflow", self.off, self.n)
        return self.ap[:, a:a + w].bitcast(BF16)[:, 0:n]


ARENA_WORDS = 53200


class Builder:
    def __init__(self, ntok, seq, layers, nrm_final=True):
        self.ntok = ntok
        self.seq = seq
        self.layers = layers
        self.nrm_final = nrm_final
        self.nc = bass.Bass("TRN2", target_bir_lowering=False)
        self.P = Prog()
        self.dram = {}

    def din(self, name, shape):
        t = self.nc.dram_tensor(name, list(shape), F32, kind="ExternalInput").ap()
        self.dram[name] = t
        return t

    def setup(self, es):
        nc = self.nc
        arena_t = es.enter_context(nc.sbuf_tensor("arena", [128, ARENA_WORDS], F32))
        self.A = Arena(arena_t, ARENA_WORDS)
        self.psum = es.enter_context(nc.psum_tensor("psum", [128, 8, 512], F32))
        A, P = self.A, self.P
        self.idf = A.f32(128)
        self.idb = A.bf16(128)
        ident = self.dram["c_ident"]
        P.add("sp", lambda e: e.dma_start(out=self.idf, in_=ident), w=["idf"], dma="c_idf")
        P.add("dve", lambda e: e.tensor_copy(out=self.idb, in_=self.idf), r=["idf"], w=["idb"])
        self.onesq = A.bf16(128)
        P.add("dve", lambda e: e.memset(self.onesq, 1.0), w=["onesq"])
        self.base_mark = A.mark()

    def bank(self, i):
        return self.psum[:, i, :]

    def dump(self, name, ap, keys):
        if not getattr(self, "debug", False):
            return
        if name in self.dram:
            return
        shp = list(ap.shape)
        t = self.nc.dram_tensor("dbg_" + name, shp, ap.dtype, kind="ExternalOutput").ap()
        self.dram[name] = t
        self.P.add("sp", lambda e: e.dma_start(out=t, in_=ap), r=list(keys), w=[("dbg", name)], dma=("dbg", name))

    def bank_bf(self, i):
        return self.psum[:, i, :].bitcast(BF16)

    def load_bcast(self, dst, src_row, key):
        self.P.add("sp", lambda e: e.dma_start(out=dst, in_=src_row.partition_broadcast(128)), w=[key], dma=key)

    def load_w(self, dst, src, key, nsplit):
        n = dst.shape[2]
        nsplit = max(1, min(nsplit, n // 128))
        step = n // nsplit
        if not hasattr(self, "wsplit"):
            self.wsplit = {}
        self.wsplit[key] = (n, step, (dst.shape[1] + 7) // 8)
        K = dst.shape[1]
        for j in range(nsplit):
            c0, c1 = j * step, (n if j == nsplit - 1 else (j + 1) * step)
            for k0 in range(0, K, 8):
                k1 = min(K, k0 + 8)
                self.P.add("pool", lambda e, c0=c0, c1=c1, k0=k0, k1=k1: e.dma_start(out=dst[:, k0:k1, c0:c1],
                                                                                   in_=src[:, k0:k1, c0:c1]),
                           w=[(key, j, k0 // 8)], dma=(key, j, k0 // 8))

    def wk(self, key, c0, n):
        tot, step, nk = self.wsplit[key]
        j0 = min(c0 // step, (tot - 1) // step)
        j1 = min((c0 + n - 1) // step, (tot - 1) // step)
        nmax = (tot + step - 1) // step - 1
        return [(key, min(j, nmax), kk) for j in range(j0, j1 + 1) for kk in range(nk)]

    def norm_tile(self, xt, xkey, gt, gkey, hb, hbkey, scr):
        P = self.P
        junk, ss, rs, sk = scr["junk"], scr["ss"], scr["rs"], scr["key"]
        jkey = scr.get("jkey", sk + "j")
        P.add("act", lambda e: e.activation(out=junk, in_=xt, func=AF.Square, accum_out=ss), r=[xkey], w=[jkey, sk + "ss"])
        P.add("dve", lambda e: e.tensor_scalar(out=ss, in0=ss, scalar1=1.0 / D, scalar2=EPS, op0=ALU.mult, op1=ALU.add),
              r=[sk + "ss"], w=[sk + "ss"])
        P.add("act", lambda e: e.activation(out=rs, in_=ss, func=AF.Sqrt), r=[sk + "ss"], w=[sk + "rs"])
        P.add("dve", lambda e: e.reciprocal(out=rs, in_=rs), r=[sk + "rs"], w=[sk + "rs"])
        P.add("dve", lambda e: e.scalar_tensor_tensor(out=hb, in0=xt, scalar=rs, in1=gt, op0=ALU.mult, op1=ALU.mult),
              r=[xkey, sk + "rs", gkey], w=[hbkey])

    def transpose_tile(self, hb, hbkey, hT, hTkey, col0, pbank, nchunk=8):
        P = self.P
        pT = self.bank_bf(pbank)
        pk = ("ps", pbank)
        for k in range(nchunk):
            P.add("pe", lambda e, k=k: e.transpose(out=pT[:, k * 128:(k + 1) * 128], in_=hb[:, k * 128:(k + 1) * 128],
                                                    identity=self.idb), r=[hbkey, "idb"], w=[pk])
        P.add("act", lambda e: e.copy(out=hT[:, 0:nchunk, col0:col0 + 128],
                                       in_=pT[:, 0:nchunk * 128].rearrange("p (k n) -> p k n", k=nchunk)),
              r=[pk], w=[hTkey])

    def ffn(self, L, half, xn, xa, xo, final_g=None):
        nc, P, A = self.nc, self.P, self.A
        P.barrier()
        A.reset(self.base_mark)
        TB = 512
        nblk = self.ntok // TB
        w1 = A.bf16(8 * 2048).rearrange("p (k n) -> p k n", k=8)
        w2 = A.bf16(16 * 1024).rearrange("p (k n) -> p k n", k=16)
        gt = A.f32(1024)
        w1d = self.dram["ffn_w1"][L][:, half * 2048:(half + 1) * 2048].rearrange("(k p) n -> p k n", p=128)
        w2d = self.dram["ffn_w2"][L][half * 2048:(half + 1) * 2048, :].rearrange("(k p) n -> p k n", p=128)
        self.load_w(w1, w1d, "w1", 4)
        self.load_w(w2, w2d, "w2", 2)
        self.load_bcast(gt, self.dram["ffn_norm_g"][L:L + 1, :], "gt")
        same = xa is xn
        xt = [[A.f32(1024) for t in range(4)] for s in range(2)]
        xat = xt if same else [[A.f32(1024) for t in range(4)] for s in range(2)]
        hb = [A.bf16(1024) for s in range(2)]
        hT = [A.bf16(8 * TB).rearrange("p (k n) -> p k n", k=8) for s in range(2)]
        scr = [dict(junk=A.f32(1024), ss=A.f32(1), rs=A.f32(1), key="scr%d" % s) for s in range(2)]
        rr = [A.f32(512) for s in range(2)]
        uT = A.bf16(16 * TB).rearrange("p (k n) -> p k n", k=16)
        xot = [A.f32(1024) for s in range(2)]

        def prologue(b):
            s = b % 2
            for t in range(4):
                row = b * TB + t * 128
                P.add("sp", lambda e, t=t, row=row: e.dma_start(out=xt[s][t], in_=xn[row:row + 128, :]),
                      w=[("xt", s, t)], dma=("xt", s, t))
                if not same:
                    P.add("sp", lambda e, t=t, row=row: e.dma_start(out=xat[s][t], in_=xa[row:row + 128, :]),
                          w=[("xat", s, t)], dma=("xat", s, t))
                q = t % 2
                self.norm_tile(xt[s][t], ("xt", s, t), gt, "gt", hb[q], ("hb", q), scr[q])
                self.transpose_tile(hb[q], ("hb", q), hT[s], ("hT", s), t * 128, 0)

        def ffn1(b):
            s = b % 2
            for m in range(16):
                bk = 1 + m % 2
                for k in range(8):
                    P.add("pe", lambda e, m=m, k=k, bk=bk: e.matmul(self.bank(bk), lhsT=w1[:, k, m * 128:(m + 1) * 128],
                                                                   rhs=hT[s][:, k, :], start=(k == 0), stop=(k == 7)),
                          r=self.wk("w1", m * 128, 128) + [("hT", s)], w=[("ps", bk)])
                q = m % 2
                P.add("act", lambda e, bk=bk, q=q: e.activation(out=rr[q], in_=self.bank(bk), func=AF.Relu),
                      r=[("ps", bk)], w=[("rr", q)])
                P.add("dve", lambda e, m=m, q=q: e.tensor_tensor(out=uT[:, m, :], in0=rr[q], in1=rr[q], op=ALU.mult),
                      r=[("rr", q)], w=[("uT", m)])

        def ffn2(b):
            s = b % 2
            for t in range(4):
                row = b * TB + t * 128
                q = t % 2
                for nh in range(2):
                    bk = 3 + nh
                    for m in range(16):
                        P.add("pe", lambda e, m=m, t=t, nh=nh, bk=bk: e.matmul(
                            self.bank(bk), lhsT=uT[:, m, t * 128:(t + 1) * 128], rhs=w2[:, m, nh * 512:(nh + 1) * 512],
                            start=(m == 0), stop=(m == 15)), r=[("uT", m)] + self.wk("w2", nh * 512, 512), w=[("ps", bk)])
                    P.add("dve", lambda e, t=t, nh=nh, bk=bk, q=q: e.tensor_tensor(
                        out=xot[q][:, nh * 512:(nh + 1) * 512], in0=self.bank(bk), in1=xat[s][t][:, nh * 512:(nh + 1) * 512],
                        op=ALU.add), r=[("ps", bk), ("xt" if same else "xat", s, t)], w=[("xot", q)])
                P.add("sp", lambda e, row=row, q=q: e.dma_start(out=xo[row:row + 128, :], in_=xot[q]),
                      r=[("xot", q)], w=[("dram", id(xo), row)], dma=("xot", q))

        prologue(0)
        for b in range(nblk):
            ffn1(b)
            if b + 1 < nblk:
                prologue(b + 1)
            ffn2(b)

    def final(self, xn, out):
        P, A = self.P, self.A
        P.barrier()
        A.reset(self.base_mark)
        gt = A.f32(1024)
        self.load_bcast(gt, self.dram["final_norm_g"].rearrange("(o n) -> o n", o=1), "gt")
        xt = [A.f32(1024) for s in range(2)]
        ot = [A.f32(1024) for s in range(2)]
        scr = [dict(junk=A.f32(1024), ss=A.f32(1), rs=A.f32(1), key="scr%d" % s) for s in range(2)]
        for i in range(self.ntok // 128):
            s = i % 2
            row = i * 128
            P.add("sp", lambda e, s=s, row=row: e.dma_start(out=xt[s], in_=xn[row:row + 128, :]), w=[("xt", s)], dma=("xt", s))
            self.norm_tile(xt[s], ("xt", s), gt, "gt", ot[s], ("ot", s), scr[s])
            P.add("sp", lambda e, s=s, row=row: e.dma_start(out=out[row:row + 128, :], in_=ot[s]), r=[("ot", s)],
                  w=[("dram", "out", row)], dma=("ot", s))


PARAM_SHAPES = {
    "even_w_in": (2, 1024, 4624), "even_w_out": (2, 2048, 1024), "ssd_conv_w": (2, 4, 1536), "ssd_conv_b": (2, 1536),
    "ssd_dt_bias": (2, 16), "ssd_a_log": (2, 16), "ssd_d": (2, 16), "ssd_norm_g": (2, 1024), "gmlp_ln_g": (2, 1024),
    "gmlp_ln_b": (2, 1024), "gmlp_w_s": (2, 8, 128, 128), "gmlp_b_s": (2, 8, 128), "odd_w_in": (2, 1024, 6144),
    "odd_w_out": (2, 2048, 1024), "conf_conv_w": (2, 31, 1024), "conf_conv_b": (2, 1024), "conf_ln_g": (2, 1024),
    "conf_ln_b": (2, 1024), "hgrn_lb_logits": (2, 1024), "hgrn_norm_g": (2, 1024), "mix_norm_g": (4, 1024),
    "ffn_norm_g": (4, 1024), "ffn_w1": (4, 1024, 4096), "ffn_w2": (4, 4096, 1024), "final_norm_g": (1024,),
}


def make_consts():
    c = {}
    c["c_ident"] = np.eye(128, dtype=np.float32)
    c["c_maskT"] = np.triu(np.ones((128, 128), dtype=np.float32))
    rm = np.ones((128, 512), dtype=np.float32)
    rm[:, ::64] = 0.0
    c["c_rmask"] = rm
    c["c_strict"] = np.tril(np.ones((128, 128), dtype=np.float32), -1)
    si = np.arange(128)[:, None]
    ti = np.arange(128)[None, :]
    am = ((si // 64 == ti // 64) & (si <= ti)).astype(np.float32)
    c["c_amask4"] = np.tile(am, (1, 4))
    return c


DEBUG = False
SCHEDULE = True
FFN_MERGED = True
FUSE_FINAL = True
PRIO_CP = True
ACT_TABLE_AWARE = True


def build(ntok, seq, plan):
    B = Builder(ntok, seq, None)
    B.debug = DEBUG
    nc = B.nc
    x_in = B.din("x", (ntok, D))
    for k, shp in PARAM_SHAPES.items():
        B.din(k, shp)
    for k, v in make_consts().items():
        B.din(k, v.shape)
    out = nc.dram_tensor("out", [ntok, D], F32, kind="ExternalOutput").ap()
    scr = [nc.dram_tensor("xs%d" % i, [ntok, D], F32).ap() for i in range(3)]
    with ExitStack() as es:
        B.setup(es)
        cur = x_in

        def free(*used):
            for s in scr:
                if all(s is not u for u in used):
                    return s

        skip_final = False
        for pi, ph in enumerate(plan):
            kind = ph[0]
            if kind == "final" and skip_final:
                continue
            if kind == "ffn":
                L = ph[1]
                if FFN_MERGED:
                    if FUSE_FINAL and pi + 1 < len(plan) and plan[pi + 1][0] == "final":
                        B.ffn_full(L, cur, None, final_out=out)
                        skip_final = True
                        continue
                    b1 = free(cur)
                    B.ffn_full(L, cur, b1)
                    cur = b1
                else:
                    b1 = free(cur)
                    B.ffn(L, 0, cur, cur, b1)
                    b2 = free(cur, b1)
                    B.ffn(L, 1, cur, b1, b2)
                    cur = b2
            elif kind in ("ssd", "gmlp", "conf", "hgrn"):
                L, xn = ph[1], ph[2]
                if xn is None:
                    b1 = free(cur)
                    getattr(B, kind)(L, cur, cur, b1)
                    B.mix_src = cur
                    cur = b1
                else:
                    b2 = free(cur, B.mix_src)
                    getattr(B, kind)(L, B.mix_src, cur, b2)
                    cur = b2
            elif kind == "final":
                B.final(cur, out)
            elif kind == "copy":
                B.P.barrier()
                B.P.add("sp", lambda e, cur=cur: e.dma_start(out=out, in_=cur), w=["out"], dma="outcopy")
        if SCHEDULE:
            B.P.schedule()
        B.P.emit(nc)
    return nc


def full_plan():
    plan = []
    for L in range(4):
        if L % 2 == 0:
            plan += [("ssd", L, None), ("gmlp", L, 1)]
        else:
            plan += [("conf", L, None), ("hgrn", L, 1)]
        plan.append(("ffn", L))
    plan.append(("final",))
    return plan


def run(x, params, plan, ncores=8, seq=None):
    bsz, s, _ = x.shape
    per = bsz // ncores
    ntok = per * s
    nc = build(ntok, s, plan)
    consts = make_consts()
    in_maps = []
    for c in range(ncores):
        m = {"x": np.ascontiguousarray(x[c * per:(c + 1) * per].reshape(ntok, D))}
        m.update(params)
        m.update(consts)
        in_maps.append(m)
    res = bu.run_bass_kernel_spmd(nc, in_maps, core_ids=list(range(ncores)))
    return np.concatenate([r["out"].reshape(per, s, D) for r in res.results], axis=0), res


def kernel(**inputs):
    x = np.ascontiguousarray(inputs["x"], dtype=np.float32)
    params = {k: np.ascontiguousarray(inputs[k], dtype=np.float32) for k in PARAM_SHAPES}
    out, _ = run(x, params, full_plan(), ncores=8)
    return out.astype(np.float32)


TB = 512


def _alloc_blk(self, same, nslot=1, junk=None):
    A = self.A
    st = {}
    nx = 4 if same else 2
    st["nx"] = nx
    st["xt"] = [[A.f32(1024) for t in range(nx)] for s in range(nslot)]
    st["xat"] = st["xt"] if same else [[A.f32(1024) for t in range(4)] for s in range(nslot)]
    st["xak"] = "xt" if same else "xat"
    st["hb"] = [A.bf16(1024) for s in range(2)]
    st["hT"] = [A.bf16(8 * TB).rearrange("p (k n) -> p k n", k=8) for s in range(nslot)]
    if junk is None:
        junk = (A.f32(1024), "junk")
    st["scr"] = [dict(junk=junk[0], jkey=junk[1], ss=A.f32(1), rs=A.f32(1), key="scr%d" % s) for s in range(2)]
    st["same"] = same
    st["nslot"] = nslot
    return st


def _prologue(self, b, st, xn, xa, gt):
    P = self.P
    s = b % st["nslot"]
    nx = st["nx"]
    for t in range(4):
        row = b * TB + t * 128
        xb = st["xt"][s][t % nx]
        xk = ("xt", s, t % nx)
        P.add("sp", lambda e, xb=xb, row=row: e.dma_start(out=xb, in_=xn[row:row + 128, :]), w=[xk], dma=xk)
        if not st["same"]:
            P.add("sp", lambda e, t=t, row=row: e.dma_start(out=st["xat"][s][t], in_=xa[row:row + 128, :]),
                  w=[("xat", s, t)], dma=("xat", s, t))
        q = t % 2
        self.norm_tile(xb, xk, gt, "gt", st["hb"][q], ("hb", q), st["scr"][q])
        self.transpose_tile(st["hb"][q], ("hb", q), st["hT"][s], ("hT", s), t * 128, 0)


def _mm_fm(self, bk, W, wkey, c0, hT, hTkey):
    for k in range(8):
        self.P.add("pe", lambda e, k=k: e.matmul(self.bank(bk), lhsT=W[:, k, c0:c0 + 128], rhs=hT[:, k, :],
                                                 start=(k == 0), stop=(k == 7)), r=self.wk(wkey, c0, 128) + [hTkey], w=[("ps", bk)])


def _mm_tm(self, bk, hT, hTkey, t, W, wkey, c0, n=512):
    for k in range(8):
        self.P.add("pe", lambda e, k=k: e.matmul(self.bank(bk)[:, 0:n], lhsT=hT[:, k, t * 128:(t + 1) * 128],
                                                 rhs=W[:, k, c0:c0 + n], start=(k == 0), stop=(k == 7)),
                   r=self.wk(wkey, c0, n) + [hTkey], w=[("ps", bk)])


def _out_proj(self, b, st, mixT, mkey, wo, xo, kc=8):
    P = self.P
    s = b % st["nslot"]
    for t in range(4):
        row = b * TB + t * 128
        q = t % 2
        for nh in range(2):
            bk = 7 if nh == 0 else 0
            for k in range(kc):
                P.add("pe", lambda e, k=k, t=t, nh=nh, bk=bk: e.matmul(
                    self.bank(bk), lhsT=mixT[:, k, t * 128:(t + 1) * 128], rhs=wo[:, k, nh * 512:(nh + 1) * 512],
                    start=(k == 0), stop=(k == kc - 1)), r=[mkey] + self.wk("wo", nh * 512, 512), w=[("ps", bk)])
            P.add("dve", lambda e, t=t, nh=nh, bk=bk, q=q: e.tensor_tensor(
                out=st["xat"][s][t][:, nh * 512:(nh + 1) * 512], in0=self.bank(bk),
                in1=st["xat"][s][t][:, nh * 512:(nh + 1) * 512], op=ALU.add),
                r=[("ps", bk), (st["xak"], s, t)], w=[(st["xak"], s, t)])
        P.add("sp", lambda e, row=row, t=t: e.dma_start(out=xo[row:row + 128, :], in_=st["xat"][s][t]),
              r=[(st["xak"], s, t)], w=[("dram", id(xo), row)], dma=("xst", s, t))


def _load_cols(self, dst, src_rows, n, key, nch=8, bk=0):
    P, A = self.P, self.A
    rows = A.f32(nch * 128)
    rk = key + "_rows"
    P.add("sp", lambda e: e.dma_start(out=rows[0:n, :], in_=src_rows), w=[rk], dma=rk)
    for m in range(nch):
        P.add("pe", lambda e, m=m: e.matmul(self.bank(bk)[:, m * n:(m + 1) * n], lhsT=rows[0:n, m * 128:(m + 1) * 128],
                                            rhs=self.idf[0:n, 0:n], start=True, stop=True),
              r=[rk, "idf"], w=[("ps", bk)])
    P.add("dve", lambda e: e.tensor_copy(out=dst, in_=self.bank(bk)[:, 0:nch * n].rearrange("p (m j) -> p m j", m=nch)),
          r=[("ps", bk)], w=[key])


def _w_in_view(self, name, idx, c0, n):
    return self.dram[name][idx][:, c0:c0 + n].rearrange("(k p) n -> p k n", p=128)


def _gmlp(self, L, xn, xa, xo):
    P, A = self.P, self.A
    e_ = L // 2
    P.barrier()
    A.reset(self.base_mark)
    C0 = 2576
    wu = A.bf16(8 * 1024).rearrange("p (k n) -> p k n", k=8)
    wv = A.bf16(8 * 1024).rearrange("p (k n) -> p k n", k=8)
    wo = A.bf16(8 * 1024).rearrange("p (k n) -> p k n", k=8)
    self.load_w(wu, _w_in_view(self, "even_w_in", e_, C0, 1024), "wu", 4)
    self.load_w(wv, _w_in_view(self, "even_w_in", e_, C0 + 1024, 1024), "wv", 4)
    self.load_w(wo, self.dram["even_w_out"][e_][1024:2048, :].rearrange("(k p) n -> p k n", p=128), "wo", 4)
    gt = A.f32(1024)
    lg = A.f32(1024)
    lb = A.f32(1024)
    bs = A.f32(1024)
    self.load_bcast(gt, self.dram["mix_norm_g"][L:L + 1, :], "gt")
    self.load_bcast(lg, self.dram["gmlp_ln_g"][e_:e_ + 1, :], "lg")
    self.load_bcast(lb, self.dram["gmlp_ln_b"][e_:e_ + 1, :], "lb")
    self.load_bcast(bs, self.dram["gmlp_b_s"][e_:e_ + 1].rearrange("o g t -> o (g t)"), "bs")
    cm = A.f32(128)
    P.add("sp", lambda e: e.dma_start(out=cm, in_=self.dram["c_maskT"]), w=["cm"], dma="cm")
    wsT = A.bf16(1024).rearrange("p (g t) -> p g t", g=8)
    mk = A.mark()
    wsf = A.f32(1024)
    wsb = A.bf16(1024)
    P.add("sp", lambda e: e.dma_start(out=wsf.rearrange("p (g s) -> p g s", g=8),
                                      in_=self.dram["gmlp_w_s"][e_].rearrange("g t s -> t g s")), w=["wsf"], dma="wsf")
    P.add("dve", lambda e: e.tensor_copy(out=wsb, in_=wsf), r=["wsf"], w=["wsb"])
    pT = self.bank_bf(0)
    for g in range(8):
        P.add("pe", lambda e, g=g: e.transpose(out=pT[:, g * 128:(g + 1) * 128], in_=wsb[:, g * 128:(g + 1) * 128],
                                                identity=self.idb), r=["wsb", "idb"], w=[("ps", 0)])
    cmb = A.bf16(128)
    P.add("dve", lambda e: e.tensor_copy(out=cmb, in_=cm), r=["cm"], w=["cmb"])
    P.add("act", lambda e: e.copy(out=wsT.rearrange("p g t -> p (g t)"), in_=pT), r=[("ps", 0)], w=["wsT"])
    for g in range(8):
        P.add("dve", lambda e, g=g: e.tensor_tensor(out=wsT[:, g, :], in0=wsT[:, g, :], in1=cmb, op=ALU.mult),
              r=["wsT", "cmb"], w=["wsT"])
    P.barrier(keep=("wu", "wv", "wo"))
    A.reset(mk)
    st = _alloc_blk(self, xa is xn)
    u = A.f32(8 * TB).rearrange("p (k n) -> p k n", k=8)
    v = [A.f32(1024) for i in range(2)]
    junk = A.f32(1024)
    vnb = [A.bf16(1024) for i in range(2)]
    stt = [A.f32(8) for i in range(2)]
    tmp = [A.f32(512) for i in range(2)]
    mixT = A.bf16(8 * TB).rearrange("p (k n) -> p k n", k=8)
    for b in range(self.ntok // TB):
        _prologue(self, b, st, xn, xa, gt)
        hT = st["hT"][0]
        hk = ("hT", 0)
        for m in range(8):
            bk = 1 + m % 2
            _mm_fm(self, bk, wu, "wu", m * 128, hT, hk)
            P.add("act", lambda e, m=m, bk=bk: e.activation(out=u[:, m, :], in_=self.bank(bk), func=AF.Gelu),
                  r=[("ps", bk)], w=[("u", m)])
        for t in range(4):
            q = t % 2
            S = stt[q]
            sk = ("stt", q)
            for nh in range(2):
                bk = 3 + nh
                _mm_tm(self, bk, hT, hk, t, wv, "wv", nh * 512)
                P.add("act", lambda e, nh=nh, bk=bk, q=q, S=S: e.activation(
                    out=v[q][:, nh * 512:(nh + 1) * 512], in_=self.bank(bk), func=AF.Gelu, accum_out=S[:, nh:nh + 1]),
                    r=[("ps", bk)], w=[("v", q), sk])
            P.add("act", lambda e, q=q, S=S: e.activation(out=junk, in_=v[q], func=AF.Square, accum_out=S[:, 2:3]),
                  r=[("v", q)], w=["junk", sk])
            P.add("dve", lambda e, S=S: e.tensor_tensor(out=S[:, 3:4], in0=S[:, 0:1], in1=S[:, 1:2], op=ALU.add), r=[sk], w=[sk])
            P.add("dve", lambda e, S=S: e.tensor_scalar(out=S[:, 3:4], in0=S[:, 3:4], scalar1=1.0 / 1024, scalar2=0.0,
                                                       op0=ALU.mult, op1=ALU.add), r=[sk], w=[sk])
            P.add("dve", lambda e, S=S: e.tensor_tensor(out=S[:, 4:5], in0=S[:, 3:4], in1=S[:, 3:4], op=ALU.mult), r=[sk], w=[sk])
            P.add("dve", lambda e, S=S: e.scalar_tensor_tensor(out=S[:, 4:5], in0=S[:, 2:3], scalar=1.0 / 1024, in1=S[:, 4:5],
                                                              op0=ALU.mult, op1=ALU.subtract), r=[sk], w=[sk])
            P.add("dve", lambda e, S=S: e.tensor_scalar(out=S[:, 4:5], in0=S[:, 4:5], scalar1=EPS, scalar2=0.0, op0=ALU.add,
                                                       op1=ALU.add), r=[sk], w=[sk])
            P.add("act", lambda e, S=S: e.activation(out=S[:, 5:6], in_=S[:, 4:5], func=AF.Sqrt), r=[sk], w=[sk])
            P.add("dve", lambda e, S=S: e.reciprocal(out=S[:, 5:6], in_=S[:, 5:6]), r=[sk], w=[sk])
            P.add("dve", lambda e, q=q, S=S: e.tensor_scalar(out=v[q], in0=v[q], scalar1=S[:, 3:4], scalar2=S[:, 5:6],
                                                            op0=ALU.subtract, op1=ALU.mult), r=[("v", q), sk], w=[("v", q)])
            P.add("dve", lambda e, q=q: e.tensor_tensor(out=v[q], in0=v[q], in1=lg, op=ALU.mult), r=[("v", q), "lg"], w=[("v", q)])
            P.add("dve", lambda e, q=q: e.tensor_tensor(out=vnb[q], in0=v[q], in1=lb, op=ALU.add), r=[("v", q), "lb"],
                  w=[("vnb", q)])
            for g in range(8):
                bk = 5 + g // 4
                P.add("pe", lambda e, g=g, bk=bk, q=q: e.matmul(self.bank(bk)[:, (g % 4) * 128:(g % 4 + 1) * 128],
                                                               lhsT=vnb[q][:, g * 128:(g + 1) * 128], rhs=wsT[:, g, :],
                                                               start=True, stop=True), r=[("vnb", q), "wsT"], w=[("ps", bk)])
            for hf in range(2):
                bk = 5 + hf
                P.add("dve", lambda e, hf=hf, bk=bk: e.tensor_tensor(out=tmp[hf], in0=self.bank(bk),
                                                                    in1=bs[:, hf * 512:(hf + 1) * 512], op=ALU.add),
                      r=[("ps", bk), "bs"], w=[("tmp", hf)])
                P.add("dve", lambda e, hf=hf, t=t: e.tensor_tensor(
                    out=mixT[:, hf * 4:(hf + 1) * 4, t * 128:(t + 1) * 128],
                    in0=tmp[hf].rearrange("p (g n) -> p g n", g=4), in1=u[:, hf * 4:(hf + 1) * 4, t * 128:(t + 1) * 128],
                    op=ALU.mult), r=[("tmp", hf)] + [("u", hf * 4 + i) for i in range(4)], w=["mixT"])
        self.dump("hT", hT, [hk])
        self.dump("u", u, [("u", m) for m in range(8)])
        self.dump("v", v[1], [("v", 1)])
        self.dump("S", stt[1], [("stt", 1)])
        self.dump("vnb", vnb[1], [("vnb", 1)])
        self.dump("wsT", wsT, ["wsT"])
        self.dump("tmp", tmp[1], [("tmp", 1)])
        self.dump("mixT", mixT, ["mixT"])
        _out_proj(self, b, st, mixT, "mixT", wo, xo)


Builder.gmlp = _gmlp


def _conf(self, L, xn, xa, xo):
    P, A = self.P, self.A
    o_ = L // 2
    P.barrier()
    A.reset(self.base_mark)
    wa = A.bf16(8 * 1024).rearrange("p (k n) -> p k n", k=8)
    wg = A.bf16(8 * 1024).rearrange("p (k n) -> p k n", k=8)
    wo = A.bf16(8 * 1024).rearrange("p (k n) -> p k n", k=8)
    self.load_w(wa, _w_in_view(self, "odd_w_in", o_, 0, 1024), "wa", 4)
    self.load_w(wg, _w_in_view(self, "odd_w_in", o_, 1024, 1024), "wg", 4)
    self.load_w(wo, self.dram["odd_w_out"][o_][0:1024, :].rearrange("(k p) n -> p k n", p=128), "wo", 4)
    gt = A.f32(1024)
    self.load_bcast(gt, self.dram["mix_norm_g"][L:L + 1, :], "gt")
    cw = A.f32(8 * 31).rearrange("p (m j) -> p m j", m=8)
    cv = A.f32(8 * 3).rearrange("p (m j) -> p m j", m=8)
    diag = A.bf16(8 * 31 * 128).rearrange("p (m j c) -> p m j c", m=8, j=31)
    mk = A.mark()
    _load_cols(self, cw, self.dram["conf_conv_w"][o_], 31, "cw")
    rows3 = A.f32(1024)
    for j, nm in enumerate(["conf_conv_b", "conf_ln_g", "conf_ln_b"]):
        P.add("sp", lambda e, j=j, nm=nm: e.dma_start(out=rows3[j:j + 1, :], in_=self.dram[nm][o_:o_ + 1, :]),
              w=["rows3"], dma="rows3")
    for m in range(8):
        P.add("pe", lambda e, m=m: e.matmul(self.bank(1)[:, m * 3:(m + 1) * 3], lhsT=rows3[0:3, m * 128:(m + 1) * 128],
                                            rhs=self.idf[0:3, 0:3], start=True, stop=True),
              r=["rows3", "idf"], w=[("ps", 1)])
    P.add("dve", lambda e: e.tensor_copy(out=cv, in_=self.bank(1)[:, 0:24].rearrange("p (m j) -> p m j", m=8)),
          r=[("ps", 1)], w=["cv"])
    for m in range(8):
        for j in range(31):
            eng = "dve"
            P.add(eng, lambda e, m=m, j=j: e.scalar_tensor_tensor(out=diag[:, m, j, :], in0=self.idf, scalar=cw[:, m, j:j + 1],
                                                                in1=self.idf, op0=ALU.mult, op1=ALU.mult),
                  r=["idf", "cw"], w=[("diag", m, j)])
    self.dump("cw", cw, ["cw"])
    self.dump("cv", cv, ["cv"])
    self.dump("diag0", diag[:, 0, :, :], [("diag", 0, j) for j in range(31)])
    P.barrier(keep=("wa", "wg", "wo"))
    A.reset(mk)
    st = _alloc_blk(self, xa is xn)
    HAL = 30
    cin1 = A.bf16(8 * (TB + HAL)).rearrange("p (m n) -> p m n", m=8)
    cin = [cin1, cin1]
    sg = [A.f32(TB) for i in range(2)]
    hc = A.f32(8 * TB).rearrange("p (m n) -> p m n", m=8)
    hcb = [A.bf16(TB) for i in range(2)]
    hsq = [A.bf16(TB) for i in range(2)]
    mean = A.f32(TB)
    rstd = A.f32(TB)
    tq = [A.f32(TB) for i in range(2)]
    mixT = A.bf16(8 * TB).rearrange("p (k n) -> p k n", k=8)
    nblk_seq = self.seq // TB
    for b in range(self.ntok // TB):
        s = 0
        _prologue(self, b, st, xn, xa, gt)
        hT = st["hT"][0]
        hk = ("hT", 0)
        if b % nblk_seq == 0:
            P.add("pool", lambda e, s=s: e.memset(cin[s][:, :, 0:HAL], 0.0), w=[("cin", s, m) for m in range(8)])
        else:
            P.add("pool", lambda e, s=s: e.tensor_copy(out=cin[s][:, :, 0:HAL], in_=cin[s][:, :, TB:TB + HAL]),
                  r=[("cin", s, m) for m in range(8)], w=[("cin", s, m) for m in range(8)])
        for m in range(8):
            _mm_fm(self, 1, wa, "wa", m * 128, hT, hk)
            _mm_fm(self, 2, wg, "wg", m * 128, hT, hk)
            q = m % 2
            P.add("act", lambda e, q=q: e.activation(out=sg[q], in_=self.bank(2), func=AF.Sigmoid), r=[("ps", 2)], w=[("sg", q)])
            P.add("dve", lambda e, q=q, m=m, s=s: e.tensor_tensor(out=cin[s][:, m, HAL:HAL + TB], in0=self.bank(1), in1=sg[q],
                                                                 op=ALU.mult), r=[("ps", 1), ("sg", q)], w=[("cin", s, m)])
        for m in range(8):
            bk = 3 + m % 2
            q = m % 2
            for j in range(31):
                P.add("pe", lambda e, m=m, j=j, bk=bk, s=s: e.matmul(self.bank(bk), lhsT=diag[:, m, j, :],
                                                                    rhs=cin[s][:, m, j:j + TB], start=(j == 0), stop=(j == 30)),
                      r=[("diag", m, j), ("cin", s, m)], w=[("ps", bk)])
            P.add("act", lambda e, m=m, bk=bk: e.activation(out=hc[:, m, :], in_=self.bank(bk), func=AF.Identity,
                                                           bias=cv[:, m, 0:1], scale=1.0), r=[("ps", bk), "cv"], w=[("hc", m)])
            P.add("dve", lambda e, m=m, q=q: e.tensor_copy(out=hcb[q], in_=hc[:, m, :]), r=[("hc", m)], w=[("hcb", q)])
            P.add("act", lambda e, m=m, q=q: e.activation(out=hsq[q], in_=hc[:, m, :], func=AF.Square), r=[("hc", m)],
                  w=[("hsq", q)])
            P.add("pe", lambda e, m=m, q=q: e.matmul(self.bank(5), lhsT=self.onesq,
                                                    rhs=hcb[q], start=(m == 0), stop=(m == 7)), r=[("hcb", q), "onesq"], w=[("ps", 5)])
            P.add("pe", lambda e, m=m, q=q: e.matmul(self.bank(6), lhsT=self.onesq, rhs=hsq[q], start=(m == 0), stop=(m == 7)),
                  r=[("hsq", q), "onesq"], w=[("ps", 6)])
        P.add("act", lambda e: e.activation(out=mean, in_=self.bank(5), func=AF.Copy, scale=1.0 / 1024), r=[("ps", 5)], w=["mean"])
        P.add("dve", lambda e: e.tensor_tensor(out=rstd, in0=mean, in1=mean, op=ALU.mult), r=["mean"], w=["rstd"])
        P.add("dve", lambda e: e.scalar_tensor_tensor(out=rstd, in0=self.bank(6), scalar=1.0 / 1024, in1=rstd, op0=ALU.mult,
                                                      op1=ALU.subtract), r=[("ps", 6), "rstd"], w=["rstd"])
        P.add("dve", lambda e: e.tensor_scalar(out=rstd, in0=rstd, scalar1=EPS, scalar2=0.0, op0=ALU.add, op1=ALU.add),
              r=["rstd"], w=["rstd"])
        P.add("act", lambda e: e.activation(out=rstd, in_=rstd, func=AF.Sqrt), r=["rstd"], w=["rstd"])
        P.add("dve", lambda e: e.reciprocal(out=rstd, in_=rstd), r=["rstd"], w=["rstd"])
        for m in range(8):
            q = m % 2
            P.add("dve", lambda e, m=m, q=q: e.tensor_tensor(out=tq[q], in0=hc[:, m, :], in1=mean, op=ALU.subtract),
                  r=[("hc", m), "mean"], w=[("tq", q)])
            P.add("dve", lambda e, q=q: e.tensor_tensor(out=tq[q], in0=tq[q], in1=rstd, op=ALU.mult), r=[("tq", q), "rstd"],
                  w=[("tq", q)])
            P.add("act", lambda e, m=m, q=q: e.activation(out=mixT[:, m, :], in_=tq[q], func=AF.Silu, scale=cv[:, m, 1:2],
                                                         bias=cv[:, m, 2:3]), r=[("tq", q), "cv"], w=["mixT"])
        self.dump("cin", cin[s], [("cin", s, m) for m in range(8)])
        self.dump("hc", hc, [("hc", m) for m in range(8)])
        self.dump("mean", mean, ["mean"])
        self.dump("rstd", rstd, ["rstd"])
        self.dump("mixT", mixT, ["mixT"])
        _out_proj(self, b, st, mixT, "mixT", wo, xo)


Builder.conf = _conf


def _hgrn(self, L, xn, xa, xo):
    P, A = self.P, self.A
    o_ = L // 2
    U32 = mybir.dt.uint32
    P.barrier()
    A.reset(self.base_mark)
    W = {}
    for i, nm in [(1, "wf"), (0, "wq"), (2, "wi"), (3, "wz")]:
        W[nm] = A.bf16(8 * 1024).rearrange("p (k n) -> p k n", k=8)
        self.load_w(W[nm], _w_in_view(self, "odd_w_in", o_, 2048 + i * 1024, 1024), nm, 4)
    wo = A.bf16(8 * 1024).rearrange("p (k n) -> p k n", k=8)
    self.load_w(wo, self.dram["odd_w_out"][o_][1024:2048, :].rearrange("(k p) n -> p k n", p=128), "wo", 4)
    gt = A.f32(1024)
    hg = A.f32(1024)
    self.load_bcast(gt, self.dram["mix_norm_g"][L:L + 1, :], "gt")
    self.load_bcast(hg, self.dram["hgrn_norm_g"][o_:o_ + 1, :], "hg")
    rmask = A.f32(512)
    amask = A.f32(512)
    P.add("sp", lambda e: e.dma_start(out=rmask, in_=self.dram["c_rmask"]), w=["rmask"], dma="rmask")
    P.add("sp", lambda e: e.dma_start(out=amask, in_=self.dram["c_amask4"]), w=["amask"], dma="amask")
    lbc = A.f32(8)
    omlb = A.f32(8)
    mk = A.mark()
    if o_ == 0:
        P.add("dve", lambda e: e.memset(lbc, 0.0), w=["lbc"])
    else:
        lgc = A.f32(16).rearrange("p (m j) -> p m j", m=8)
        _load_cols(self, lgc, self.dram["hgrn_lb_logits"], 2, "lgc")
        P.add("dve", lambda e: e.tensor_tensor(out=lbc, in0=lgc[:, :, 1], in1=lgc[:, :, 0], op=ALU.subtract), r=["lgc"], w=["lbc"])
        P.add("act", lambda e: e.activation(out=lbc, in_=lbc, func=AF.Sigmoid), r=["lbc"], w=["lbc"])
    P.add("dve", lambda e: e.tensor_scalar(out=omlb, in0=lbc, scalar1=-1.0, scalar2=1.0, op0=ALU.mult, op1=ALU.add),
          r=["lbc"], w=["omlb"])
    P.barrier(keep=("wq", "wf", "wi", "wz", "wo"))
    A.reset(mk)
    on = A.f32(1024)
    st = _alloc_blk(self, xa is xn, junk=(on, "on"))
    qg = A.bf16(8 * TB).rearrange("p (h n) -> p h n", h=8)
    qd = A.bf16(8 * TB).rearrange("p (h n) -> p h n", h=8)
    kg = A.bf16(8 * TB).rearrange("p (h n) -> p h n", h=8)
    kdf = A.bf16(TB)
    kdT = A.bf16(4 * 8 * 128).rearrange("p (t h k) -> p t h k", t=4, h=8)
    ivb = A.bf16(4 * 1024).rearrange("p (t n) -> p t n", t=4)
    sgt1 = A.f32(1024)
    sgt = [sgt1, sgt1]
    ff = A.f32(TB)
    logf = A.f32(TB)
    kk = A.f32(TB)
    bcum = A.f32(TB)
    d1 = A.f32(TB)
    eqg = A.f32(TB)
    ekg = A.f32(TB)
    eqd = ff
    ekd = logf
    ecd = A.f32(64).rearrange("p (h c) -> p h c", h=8)
    S = A.f32(1024)
    Sb = [A.bf16(1024) for i in range(2)]
    attb = [A.bf16(1024) for i in range(2)]
    r8 = A.f32(8)
    osq = on
    yb = A.bf16(1024)
    mixT = st["hT"][0]
    MK = ("hT", 0)
    for i in range(2):
        P.add("pool", lambda e, i=i: e.memset(attb[i], 0.0), w=[("attb", i)])
    SC = float(128 ** -0.5)
    nblk_seq = self.seq // TB
    sidx = 0
    c3 = lambda ap: ap.rearrange("p (c j) -> p c j", j=64)
    for b in range(self.ntok // TB):
        _prologue(self, b, st, xn, xa, gt)
        hT = st["hT"][0]
        hk = ("hT", 0)
        if b % nblk_seq == 0:
            P.add("pool", lambda e: e.memset(S, 0.0), w=["S"])
            P.add("pool", lambda e, sidx=sidx: e.memset(Sb[sidx], 0.0), w=[("Sb", sidx)])
        for h in range(8):
            _mm_fm(self, 1, W["wf"], "wf", h * 128, hT, hk)
            P.add("act", lambda e: e.activation(out=ff, in_=self.bank(1), func=AF.Sigmoid), r=[("ps", 1)], w=["ff"])
            P.add("dve", lambda e, h=h: e.tensor_scalar(out=ff, in0=ff, scalar1=omlb[:, h:h + 1], scalar2=lbc[:, h:h + 1],
                                                       op0=ALU.mult, op1=ALU.add), r=["ff", "omlb", "lbc"], w=["ff"])
            P.add("act", lambda e: e.activation(out=logf, in_=ff, func=AF.Ln), r=["ff"], w=["logf"])
            P.add("dve", lambda e: e.tensor_scalar(out=kk, in0=ff, scalar1=-1.0, scalar2=1.0, op0=ALU.mult, op1=ALU.add),
                  r=["ff"], w=["kk"])
            P.add("dve", lambda e: e.tensor_tensor_scan(out=bcum, data0=rmask, data1=logf, initial=0.0, op0=ALU.mult,
                                                        op1=ALU.add), r=["rmask", "logf"], w=["bcum"])
            P.add("dve", lambda e: e.tensor_tensor(out=c3(d1), in0=c3(bcum), in1=c3(bcum)[:, :, 32:33].to_broadcast([128, 8, 64]),
                                                   op=ALU.subtract), r=["bcum"], w=["d1"])
            P.add("act", lambda e: e.activation(out=eqg, in_=d1, func=AF.Exp), r=["d1"], w=["eqg"])
            P.add("act", lambda e: e.activation(out=ekg, in_=d1, func=AF.Exp, scale=-1.0), r=["d1"], w=["ekg"])
            P.add("act", lambda e: e.activation(out=eqd, in_=bcum, func=AF.Exp), r=["bcum"], w=["ff"])
            P.add("dve", lambda e: e.tensor_tensor(out=c3(d1), in0=c3(bcum), in1=c3(bcum)[:, :, 63:64].to_broadcast([128, 8, 64]),
                                                   op=ALU.subtract), r=["bcum", "eqg", "ekg"], w=["d1"])
            P.add("act", lambda e: e.activation(out=ekd, in_=d1, func=AF.Exp, scale=-1.0), r=["d1"], w=["logf"])
            P.add("dve", lambda e, h=h: e.tensor_copy(out=ecd[:, h, :], in_=c3(eqd)[:, :, 63]), r=["ff"], w=["ecd"])
            _mm_fm(self, 2, W["wq"], "wq", h * 128, hT, hk)
            P.add("dve", lambda e, h=h: e.scalar_tensor_tensor(out=qg[:, h, :], in0=self.bank(2), scalar=SC, in1=eqg,
                                                              op0=ALU.mult, op1=ALU.mult), r=[("ps", 2), "eqg"], w=[("qg", h)])
            P.add("dve", lambda e, h=h: e.scalar_tensor_tensor(out=qd[:, h, :], in0=self.bank(2), scalar=SC, in1=eqd,
                                                              op0=ALU.mult, op1=ALU.mult), r=[("ps", 2), "ff"], w=[("qd", h)])
            P.add("dve", lambda e, h=h: e.tensor_tensor(out=kg[:, h, :], in0=kk, in1=ekg, op=ALU.mult), r=["kk", "ekg"],
                  w=[("kg", h)])
            P.add("dve", lambda e: e.tensor_tensor(out=kdf, in0=kk, in1=ekd, op=ALU.mult), r=["kk", "logf"], w=["kdf"])
            pT = self.bank_bf(7)
            for t in range(4):
                P.add("pe", lambda e, t=t: e.transpose(out=pT[:, t * 128:(t + 1) * 128], in_=kdf[:, t * 128:(t + 1) * 128],
                                                        identity=self.idb), r=["kdf", "idb"], w=[("ps", 7)])
            P.add("act", lambda e, h=h: e.copy(out=kdT[:, :, h, :], in_=pT[:, 0:512].rearrange("p (t k) -> p t k", t=4)),
                  r=[("ps", 7)], w=[("kdT", h)])
        if b == 0:
            self.dump("qg", qg, [("qg", h) for h in range(8)])
            self.dump("kg", kg, [("kg", h) for h in range(8)])
            self.dump("qd", qd, [("qd", h) for h in range(8)])
            self.dump("kdT", kdT, [("kdT", h) for h in range(8)])
            self.dump("ecd", ecd, ["ecd"])
        for t in range(4):
            q = t % 2
            for nh in range(2):
                _mm_tm(self, 3 + nh, hT, hk, t, W["wi"], "wi", nh * 512)
                P.add("act", lambda e, t=t, nh=nh: e.copy(out=ivb[:, t, nh * 512:(nh + 1) * 512], in_=self.bank(3 + nh)),
                      r=[("ps", 3 + nh)], w=[("ivb", t)])
            for nh in range(2):
                _mm_tm(self, 3 + nh, hT, hk, t, W["wz"], "wz", nh * 512)
                P.add("act", lambda e, q=q, nh=nh: e.activation(out=sgt[q][:, nh * 512:(nh + 1) * 512], in_=self.bank(3 + nh),
                                                               func=AF.Silu), r=[("ps", 3 + nh)], w=[("sgt", 0)])
            for h in range(8):
                bk = 5 + h // 4
                P.add("pe", lambda e, h=h, t=t, bk=bk: e.matmul(self.bank(bk)[:, (h % 4) * 128:(h % 4 + 1) * 128],
                                                               lhsT=kg[:, h, t * 128:(t + 1) * 128],
                                                               rhs=qg[:, h, t * 128:(t + 1) * 128], start=True, stop=True),
                      r=[("kg", h), ("qg", h)], w=[("ps", bk)])
            for hf in range(2):
                P.add("dve", lambda e, hf=hf, q=q: e.copy_predicated(out=attb[q][:, hf * 512:(hf + 1) * 512],
                                                                    mask=amask.bitcast(U32), data=self.bank(5 + hf)),
                      r=[("ps", 5 + hf), "amask", ("attb", q)], w=[("attb", q)])
            def state_update(c, t=t):
                nonlocal sidx
                rows = slice(c * 64, (c + 1) * 64)
                for h in range(8):
                    bk = 5 + h // 4
                    oc = slice((h % 4) * 128, (h % 4 + 1) * 128)
                    P.add("pe", lambda e, h=h, bk=bk, oc=oc, rows=rows, t=t: e.matmul(
                        self.bank(bk)[:, oc], lhsT=kdT[rows, t, h, :], rhs=ivb[rows, t, h * 128:(h + 1) * 128],
                        start=True, stop=True), r=[("kdT", h), ("ivb", t)], w=[("ps", bk)])
                cc = t * 2 + c
                P.add("dve", lambda e, cc=cc: e.tensor_tensor(out=S.rearrange("p (h v) -> p h v", h=8),
                                                              in0=S.rearrange("p (h v) -> p h v", h=8),
                                                              in1=ecd[:, :, cc:cc + 1].to_broadcast([128, 8, 128]), op=ALU.mult),
                      r=["S", "ecd"], w=["S"])
                for hf in range(2):
                    P.add("dve", lambda e, hf=hf: e.tensor_tensor(out=S[:, hf * 512:(hf + 1) * 512], in0=self.bank(5 + hf),
                                                                  in1=S[:, hf * 512:(hf + 1) * 512], op=ALU.add),
                          r=[("ps", 5 + hf), "S"], w=["S"])
                nxt = 1 - sidx
                P.add("act", lambda e, nxt=nxt: e.copy(out=Sb[nxt], in_=S), r=["S"], w=[("Sb", nxt)])
                sidx = nxt

            s0 = sidx
            state_update(0)
            s1 = sidx
            for h in range(8):
                bk = 1 + h // 4
                oc = slice((h % 4) * 128, (h % 4 + 1) * 128)
                P.add("pe", lambda e, h=h, bk=bk, oc=oc, q=q, t=t: e.matmul(
                    self.bank(bk)[:, oc], lhsT=attb[q][:, h * 128:(h + 1) * 128], rhs=ivb[:, t, h * 128:(h + 1) * 128],
                    start=True, stop=False), r=[("attb", q), ("ivb", t)], w=[("ps", bk)])
                for c, cur in ((0, s0), (1, s1)):
                    rows = slice(c * 64, (c + 1) * 64)
                    cols = slice(t * 128 + c * 64, t * 128 + (c + 1) * 64)
                    P.add("pe", lambda e, h=h, bk=bk, oc=oc, rows=rows, cols=cols, cur=cur, c=c: e.matmul(
                        self.bank(bk)[rows, oc], lhsT=qd[:, h, cols], rhs=Sb[cur][:, h * 128:(h + 1) * 128],
                        start=False, stop=(c == 1)), r=[("qd", h), ("Sb", cur)], w=[("ps", bk)])
            state_update(1)
            for hf in range(2):
                P.add("act", lambda e, hf=hf: e.activation(out=osq[:, hf * 512:(hf + 1) * 512], in_=self.bank(1 + hf),
                                                          func=AF.Square), r=[("ps", 1 + hf)], w=["on"])
            P.add("dve", lambda e: e.tensor_reduce(out=r8, in_=osq.rearrange("p (h v) -> p h v", h=8), axis=AX.X, op=ALU.add),
                  r=["on"], w=["r8"])
            P.add("dve", lambda e: e.tensor_scalar(out=r8, in0=r8, scalar1=1.0 / 128, scalar2=EPS, op0=ALU.mult, op1=ALU.add),
                  r=["r8"], w=["r8"])
            P.add("act", lambda e: e.activation(out=r8, in_=r8, func=AF.Sqrt), r=["r8"], w=["r8"])
            P.add("dve", lambda e: e.reciprocal(out=r8, in_=r8), r=["r8"], w=["r8"])
            for hf in range(2):
                P.add("dve", lambda e, hf=hf: e.tensor_tensor(
                    out=on[:, hf * 512:(hf + 1) * 512].rearrange("p (h v) -> p h v", h=4),
                    in0=self.bank(1 + hf).rearrange("p (h v) -> p h v", h=4),
                    in1=r8[:, hf * 4:(hf + 1) * 4].unsqueeze(2).to_broadcast([128, 4, 128]), op=ALU.mult),
                    r=[("ps", 1 + hf), "r8"], w=["on"])
            P.add("dve", lambda e: e.tensor_tensor(out=on, in0=on, in1=hg, op=ALU.mult), r=["on", "hg"], w=["on"])
            P.add("dve", lambda e, q=q: e.tensor_tensor(out=yb, in0=on, in1=sgt[q], op=ALU.mult), r=["on", ("sgt", 0)], w=["yb"])
            self.transpose_tile(yb, "yb", mixT, MK, t * 128, 0)
        if b == 0:
            self.dump("on", on, ["on"])
            self.dump("S", S, ["S"])
            self.dump("mixT", mixT, [MK])
        _out_proj(self, b, st, mixT, MK, wo, xo)


Builder.hgrn = _hgrn


def _ssd(self, L, xn, xa, xo):
    P, A = self.P, self.A
    e_ = L // 2
    P.barrier()
    A.reset(self.base_mark)
    wz = A.bf16(8 * 1024).rearrange("p (k n) -> p k n", k=8)
    wx = A.bf16(8 * 1536).rearrange("p (k n) -> p k n", k=8)
    wdt = A.bf16(8 * 16).rearrange("p (k n) -> p k n", k=8)
    wo = A.bf16(8 * 1024).rearrange("p (k n) -> p k n", k=8)
    self.load_w(wx, _w_in_view(self, "even_w_in", e_, 1024, 1536), "wx", 4)
    self.load_w(wdt, _w_in_view(self, "even_w_in", e_, 2560, 16), "wdt", 1)
    self.load_w(wz, _w_in_view(self, "even_w_in", e_, 0, 1024), "wz", 4)
    self.load_w(wo, self.dram["even_w_out"][e_][0:1024, :].rearrange("(k p) n -> p k n", p=128), "wo", 4)
    gt = A.f32(1024)
    ng = A.f32(1024)
    self.load_bcast(gt, self.dram["mix_norm_g"][L:L + 1, :], "gt")
    self.load_bcast(ng, self.dram["ssd_norm_g"][e_:e_ + 1, :], "ng")
    dtb = A.f32(16)
    aneg = A.f32(16)
    dsk = A.f32(16)
    self.load_bcast(dtb, self.dram["ssd_dt_bias"][e_:e_ + 1, :], "dtb")
    self.load_bcast(aneg, self.dram["ssd_a_log"][e_:e_ + 1, :], "aneg")
    self.load_bcast(dsk, self.dram["ssd_d"][e_:e_ + 1, :], "dsk")
    P.add("act", lambda e: e.activation(out=aneg, in_=aneg, func=AF.Exp), r=["aneg"], w=["aneg"])
    P.add("dve", lambda e: e.tensor_scalar(out=aneg, in0=aneg, scalar1=-1.0, scalar2=0.0, op0=ALU.mult, op1=ALU.add),
          r=["aneg"], w=["aneg"])
    maskT = A.f32(128)
    maskTb = A.bf16(128)
    strict = A.f32(128)
    onesf = A.f32(128)
    P.add("sp", lambda e: e.dma_start(out=maskT, in_=self.dram["c_maskT"]), w=["maskT"], dma="maskT")
    P.add("sp", lambda e: e.dma_start(out=strict, in_=self.dram["c_strict"]), w=["strict"], dma="strict")
    P.add("dve", lambda e: e.tensor_copy(out=maskTb, in_=maskT), r=["maskT"], w=["maskTb"])
    P.add("dve", lambda e: e.memset(onesf, 1.0), w=["onesf"])
    DI = A.bf16(16 * 128).rearrange("p (h l) -> p h l", h=16)
    for h in range(16):
        P.add("dve", lambda e, h=h: e.scalar_tensor_tensor(out=DI[:, h, :], in0=self.idf, scalar=dsk[:, h:h + 1], in1=self.idf,
                                                          op0=ALU.mult, op1=ALU.mult), r=["idf", "dsk"], w=["DI"])
    cw = A.f32(12 * 4).rearrange("p (m j) -> p m j", m=12)
    cb = A.f32(12).rearrange("p (m j) -> p m j", m=12)
    halo = A.f32(36).rearrange("p (m j) -> p m j", m=12)
    mk = A.mark()
    _load_cols(self, cw, self.dram["ssd_conv_w"][e_], 4, "cw", nch=12, bk=1)
    _load_cols(self, cb, self.dram["ssd_conv_b"][e_:e_ + 1, :], 1, "cb", nch=12, bk=2)
    P.barrier(keep=("wz", "wx", "wdt", "wo"))
    A.reset(mk)
    st = _alloc_blk(self, xa is xn)
    xr = [A.f32(TB + 3) for i in range(2)]
    acc = [A.f32(TB) for i in range(2)]
    xsb = [A.bf16(TB) for i in range(2)]
    BC = A.bf16(4 * TB).rearrange("p (m n) -> p m n", m=4)
    xs_tm = A.bf16(4 * 1024).rearrange("p (t n) -> p t n", t=4)
    Btm = A.bf16(4 * 2 * 128).rearrange("p (t g n) -> p t g n", t=4, g=2)
    zsl = [A.f32(1024) for i in range(2)]
    sml = [A.f32(16 * 10).rearrange("p (j h) -> p j h", j=10) for i in range(2)]
    wbl = [A.bf16(16) for i in range(2)]
    rsegl = [A.f32(16 * 128).rearrange("p (h l) -> p h l", h=16) for i in range(2)]
    attl = [A.bf16(16 * 128).rearrange("p (h l) -> p h l", h=16) for i in range(2)]
    cbml = [A.bf16(256).rearrange("p (g l) -> p g l", g=2) for i in range(2)]
    xwl = [A.bf16(1024) for i in range(2)]
    yl = [A.f32(1024) for i in range(2)]
    ss2l = [A.f32(4) for i in range(2)]
    state = A.f32(1024)
    stb = [A.bf16(1024) for i in range(2)]
    ybl = [A.bf16(1024) for i in range(2)]
    mixT = st["hT"][0]
    MK = ("hT", 0)
    nblk_seq = self.seq // TB
    sidx = 0
    for b in range(self.ntok // TB):
        _prologue(self, b, st, xn, xa, gt)
        hT = st["hT"][0]
        hk = ("hT", 0)
        if b % nblk_seq == 0:
            P.add("pool", lambda e: e.memset(state, 0.0), w=["state"])
            P.add("pool", lambda e, sidx=sidx: e.memset(stb[sidx], 0.0), w=[("stb", sidx)])
            P.add("pool", lambda e: e.memset(halo, 0.0), w=["halo"])
        for m in range(12):
            q = m % 2
            bk = 1 + q
            _mm_fm(self, bk, wx, "wx", m * 128, hT, hk)
            P.add("act", lambda e, q=q, bk=bk: e.copy(out=xr[q][:, 3:3 + TB], in_=self.bank(bk)), r=[("ps", bk)], w=[("xr", q)])
            P.add("pool", lambda e, q=q, m=m: e.tensor_copy(out=xr[q][:, 0:3], in_=halo[:, m, :]), r=["halo"], w=[("xr", q)])
            P.add("act", lambda e, q=q, m=m: e.activation(out=acc[q], in_=xr[q][:, 3:3 + TB], func=AF.Identity,
                                                         scale=cw[:, m, 3:4], bias=cb[:, m, 0:1]), r=[("xr", q), "cw", "cb"],
                  w=[("acc", q)])
            for j in range(3):
                P.add("dve", lambda e, q=q, m=m, j=j: e.scalar_tensor_tensor(out=acc[q], in0=xr[q][:, j:j + TB], scalar=cw[:, m, j:j + 1],
                                                                            in1=acc[q], op0=ALU.mult, op1=ALU.add),
                      r=[("xr", q), ("acc", q), "cw"], w=[("acc", q)])
            P.add("pool", lambda e, q=q, m=m: e.tensor_copy(out=halo[:, m, :], in_=xr[q][:, TB:TB + 3]), r=[("xr", q)], w=["halo"])
            if m < 8:
                P.add("act", lambda e, q=q: e.activation(out=xsb[q], in_=acc[q], func=AF.Silu), r=[("acc", q)], w=[("xsb", q)])
                src = xsb[q]
                skey = ("xsb", q)
            else:
                P.add("act", lambda e, q=q, m=m: e.activation(out=BC[:, m - 8, :], in_=acc[q], func=AF.Silu), r=[("acc", q)],
                      w=[("BC", m - 8)])
                src = BC[:, m - 8, :]
                skey = ("BC", m - 8)
            if m < 10:
                pT = self.bank_bf(0)
                for t in range(4):
                    P.add("pe", lambda e, t=t, src=src: e.transpose(out=pT[:, t * 128:(t + 1) * 128], in_=src[:, t * 128:(t + 1) * 128],
                                                                    identity=self.idb), r=[skey, "idb"], w=[("ps", 0)])
                if m < 8:
                    P.add("act", lambda e, m=m: e.copy(out=xs_tm[:, :, m * 128:(m + 1) * 128],
                                                       in_=pT[:, 0:512].rearrange("p (t c) -> p t c", t=4)),
                          r=[("ps", 0)], w=[("xs_tm", m)])
                else:
                    P.add("act", lambda e, m=m: e.copy(out=Btm[:, :, m - 8, :], in_=pT[:, 0:512].rearrange("p (t c) -> p t c", t=4)),
                          r=[("ps", 0)], w=["Btm"])
        xsk = [("xs_tm", m) for m in range(8)]
        def tile(t):
            nonlocal sidx
            q = t % 2
            zs, sm, wb, rseg, att, cbm, xw, y, ss2, yb = (zsl[q], sml[q], wbl[q], rsegl[q], attl[q], cbml[q], xwl[q], yl[q],
                                                          ss2l[q], ybl[q])
            dt, lndt, dA, acs, tot, dte, dfs, cdec, w32 = [sm[:, j, :] for j in range(9)]
            tc_ = slice(t * 128, (t + 1) * 128)
            for nh in range(2):
                _mm_tm(self, 5 + nh, hT, hk, t, wz, "wz", nh * 512)
                P.add("act", lambda e, nh=nh: e.activation(out=zs[:, nh * 512:(nh + 1) * 512], in_=self.bank(5 + nh), func=AF.Silu),
                      r=[("ps", 5 + nh)], w=[("zs", q)])
            b7 = self.bank(7)
            for k in range(8):
                P.add("pe", lambda e, k=k, t=t: e.matmul(b7[:, 256:272], lhsT=hT[:, k, t * 128:(t + 1) * 128], rhs=wdt[:, k, :],
                                                        start=(k == 0), stop=(k == 7)), r=[hk, ("wdt", 0, 0)], w=[("ps", 7)])
            P.add("dve", lambda e: e.tensor_tensor(out=dt, in0=b7[:, 256:272], in1=dtb, op=ALU.add), r=[("ps", 7), "dtb"], w=[("sm", q)])
            P.add("act", lambda e: e.activation(out=dt, in_=dt, func=AF.Exp), r=[("sm", q)], w=[("sm", q)])
            P.add("act", lambda e: e.activation(out=dt, in_=dt, func=AF.Ln, bias=1.0, scale=1.0), r=[("sm", q)], w=[("sm", q)])
            P.add("act", lambda e: e.activation(out=lndt, in_=dt, func=AF.Ln), r=[("sm", q)], w=[("sm", q)])
            P.add("dve", lambda e: e.tensor_tensor(out=dA, in0=dt, in1=aneg, op=ALU.mult), r=[("sm", q), "aneg"], w=[("sm", q)])
            P.add("pe", lambda e: e.matmul(b7[:, 272:288], lhsT=maskT, rhs=dA, start=True, stop=True), r=["maskT", ("sm", q)], w=[("ps", 7)])
            P.add("pe", lambda e: e.matmul(b7[:, 288:304], lhsT=onesf, rhs=dA, start=True, stop=True), r=["onesf", ("sm", q)], w=[("ps", 7)])
            P.add("act", lambda e: e.copy(out=acs, in_=b7[:, 272:288]), r=[("ps", 7)], w=[("sm", q)])
            P.add("act", lambda e: e.copy(out=tot, in_=b7[:, 288:304]), r=[("ps", 7)], w=[("sm", q)])
            P.add("dve", lambda e: e.tensor_tensor(out=dte, in0=tot, in1=acs, op=ALU.subtract), r=[("sm", q)], w=[("sm", q)])
            P.add("act", lambda e: e.activation(out=dte, in_=dte, func=AF.Exp), r=[("sm", q)], w=[("sm", q)])
            P.add("act", lambda e: e.activation(out=dfs, in_=acs, func=AF.Exp), r=[("sm", q)], w=[("sm", q)])
            P.add("act", lambda e: e.activation(out=cdec, in_=tot, func=AF.Exp), r=[("sm", q)], w=[("sm", q)])
            P.add("dve", lambda e: e.tensor_tensor(out=wb, in0=dte, in1=dt, op=ALU.mult), r=[("sm", q)], w=[("wb", q)])
            P.add("dve", lambda e: e.tensor_tensor(out=rseg, in0=maskT.unsqueeze(1).to_broadcast([128, 16, 128]),
                                                   in1=dA.unsqueeze(2).to_broadcast([128, 16, 128]), op=ALU.mult),
                  r=["maskT", ("sm", q)], w=[("rseg", q)])
            for j in range(4):
                sb_ = 1 + j % 2
                P.add("pe", lambda e, j=j, sb_=sb_: e.matmul(self.bank(sb_), lhsT=strict, rhs=rseg[:, j * 4:(j + 1) * 4, :].rearrange("p h l -> p (h l)"),
                                                            start=True, stop=True), r=["strict", ("rseg", q)], w=[("ps", sb_)])
                for h in range(j * 4, j * 4 + 4):
                    P.add("act", lambda e, h=h, sb_=sb_: e.activation(out=att[:, h, :], in_=self.bank(sb_)[:, (h % 4) * 128:(h % 4 + 1) * 128],
                                                                     func=AF.Exp, bias=lndt[:, h:h + 1], scale=1.0),
                          r=[("ps", sb_), ("sm", q)], w=[("att", q)])
            for g in range(2):
                P.add("pe", lambda e, g=g, tc_=tc_: e.matmul(b7[:, g * 128:(g + 1) * 128], lhsT=BC[:, g, tc_], rhs=BC[:, 2 + g, tc_],
                                                            start=True, stop=True), r=[("BC", g), ("BC", 2 + g)], w=[("ps", 7)])
            P.add("act", lambda e: e.copy(out=cbm.rearrange("p g l -> p (g l)"), in_=b7[:, 0:256]), r=[("ps", 7)], w=[("cbm", q)])
            for g in range(2):
                P.add("dve", lambda e, g=g: e.tensor_tensor(out=cbm[:, g, :], in0=cbm[:, g, :], in1=maskTb, op=ALU.mult),
                      r=[("cbm", q), "maskTb"], w=[("cbm", q)])
            for g in range(2):
                P.add("dve", lambda e, g=g: e.tensor_tensor(out=att[:, g * 8:(g + 1) * 8, :], in0=att[:, g * 8:(g + 1) * 8, :],
                                                            in1=cbm[:, g, :].unsqueeze(1).to_broadcast([128, 8, 128]), op=ALU.mult),
                      r=[("att", q), ("cbm", q)], w=[("att", q)])
            P.add("dve", lambda e: e.tensor_tensor(out=att, in0=att, in1=DI, op=ALU.add), r=[("att", q), "DI"], w=[("att", q)])
            cur = sidx
            for h in range(16):
                bk = 5 + h // 8
                P.add("pe", lambda e, h=h, bk=bk, t=t: e.matmul(self.bank(bk)[:, (h % 8) * 64:(h % 8 + 1) * 64], lhsT=att[:, h, :],
                                                               rhs=xs_tm[:, t, h * 64:(h + 1) * 64], start=True, stop=True),
                      r=[("att", q)] + xsk, w=[("ps", bk)])
            for g in range(2):
                P.add("pe", lambda e, g=g, tc_=tc_, cur=cur: e.matmul(self.bank(3 + g), lhsT=BC[:, 2 + g, tc_],
                                                                     rhs=stb[cur][:, g * 512:(g + 1) * 512], start=True, stop=True),
                      r=[("BC", 2 + g), ("stb", cur)], w=[("ps", 3 + g)])
            for g in range(2):
                P.add("dve", lambda e, g=g: e.tensor_tensor(out=y[:, g * 512:(g + 1) * 512].rearrange("p (h v) -> p h v", h=8),
                                                            in0=self.bank(3 + g).rearrange("p (h v) -> p h v", h=8),
                                                            in1=dfs[:, g * 8:(g + 1) * 8].unsqueeze(2).to_broadcast([128, 8, 64]),
                                                            op=ALU.mult), r=[("ps", 3 + g), ("sm", q)], w=[("y", q)])
                P.add("dve", lambda e, g=g: e.tensor_tensor(out=y[:, g * 512:(g + 1) * 512], in0=self.bank(5 + g),
                                                            in1=y[:, g * 512:(g + 1) * 512], op=ALU.add), r=[("ps", 5 + g), ("y", q)], w=[("y", q)])
            P.add("dve", lambda e, t=t: e.tensor_tensor(out=xw.rearrange("p (h v) -> p h v", h=16),
                                                        in0=xs_tm[:, t, :].rearrange("p (h v) -> p h v", h=16),
                                                        in1=wb.unsqueeze(2).to_broadcast([128, 16, 64]), op=ALU.mult),
                  r=xsk + [("wb", q)], w=[("xw", q)])
            for g in range(2):
                P.add("pe", lambda e, g=g, t=t: e.matmul(self.bank(3 + g), lhsT=Btm[:, t, g, :], rhs=xw[:, g * 512:(g + 1) * 512],
                                                        start=True, stop=True), r=["Btm", ("xw", q)], w=[("ps", 3 + g)])
            P.add("dve", lambda e: e.tensor_tensor(out=state.rearrange("p (h v) -> p h v", h=16),
                                                   in0=state.rearrange("p (h v) -> p h v", h=16),
                                                   in1=cdec.unsqueeze(2).to_broadcast([128, 16, 64]), op=ALU.mult),
                  r=["state", ("sm", q)], w=["state"])
            for g in range(2):
                P.add("dve", lambda e, g=g: e.tensor_tensor(out=state[:, g * 512:(g + 1) * 512], in0=self.bank(3 + g),
                                                            in1=state[:, g * 512:(g + 1) * 512], op=ALU.add),
                      r=[("ps", 3 + g), "state"], w=["state"])
            nxt = 1 - sidx
            P.add("act", lambda e, nxt=nxt: e.copy(out=stb[nxt], in_=state), r=["state"], w=[("stb", nxt)])
            sidx = nxt
            P.add("dve", lambda e: e.tensor_tensor(out=y, in0=y, in1=zs, op=ALU.mult), r=[("y", q), ("zs", q)], w=[("y", q)])
            jk = st["scr"][0]["junk"]
            jkk = st["scr"][0]["jkey"]
            for g in range(2):
                P.add("act", lambda e, g=g: e.activation(out=jk[:, g * 512:(g + 1) * 512], in_=y[:, g * 512:(g + 1) * 512],
                                                        func=AF.Square, accum_out=ss2[:, g:g + 1]), r=[("y", q)], w=[jkk, ("ss2", q)])
            P.add("dve", lambda e: e.tensor_scalar(out=ss2[:, 0:2], in0=ss2[:, 0:2], scalar1=1.0 / 512, scalar2=EPS, op0=ALU.mult,
                                                   op1=ALU.add), r=[("ss2", q)], w=[("ss2", q)])
            P.add("act", lambda e: e.activation(out=ss2[:, 0:2], in_=ss2[:, 0:2], func=AF.Sqrt), r=[("ss2", q)], w=[("ss2", q)])
            P.add("dve", lambda e: e.reciprocal(out=ss2[:, 0:2], in_=ss2[:, 0:2]), r=[("ss2", q)], w=[("ss2", q)])
            for g in range(2):
                P.add("dve", lambda e, g=g: e.scalar_tensor_tensor(out=yb[:, g * 512:(g + 1) * 512], in0=y[:, g * 512:(g + 1) * 512],
                                                                  scalar=ss2[:, g:g + 1], in1=ng[:, g * 512:(g + 1) * 512],
                                                                  op0=ALU.mult, op1=ALU.mult), r=[("y", q), ("ss2", q), "ng"], w=[("yb", q)])
            self.transpose_tile(yb, ("yb", q), mixT, MK, t * 128, 0)
            if b == 0 and t == 1:
                self.dump(("sm", q), sm, [("sm", q)])
                self.dump(("att", q), att, [("att", q)])
                self.dump(("y", q), y, [("y", q)])
                self.dump("state", state, ["state"])
                self.dump("xs_tm", xs_tm, xsk)
                self.dump("BC", BC, [("BC", i) for i in range(4)])
        for t in range(4):
            tile(t)
        _out_proj(self, b, st, mixT, MK, wo, xo)


Builder.ssd = _ssd


def _ffn_full(self, L, xn, xo, final_out=None):
    P, A = self.P, self.A
    P.barrier()
    A.reset(self.base_mark)
    NM = DFF // 128
    w1 = A.bf16(8 * DFF).rearrange("p (k n) -> p k n", k=8)
    w2 = A.bf16(NM * 1024).rearrange("p (k n) -> p k n", k=NM)
    gt = A.f32(1024)
    self.load_w(w1, self.dram["ffn_w1"][L].rearrange("(k p) n -> p k n", p=128), "w1", 8)
    self.load_w(w2, self.dram["ffn_w2"][L].rearrange("(k p) n -> p k n", p=128), "w2", 4)
    self.load_bcast(gt, self.dram["ffn_norm_g"][L:L + 1, :], "gt")
    xt = [A.f32(1024) for i in range(2)]
    xa = [A.f32(1024) for i in range(2)]
    if final_out is None:
        hb = [A.bf16(1024) for i in range(2)]
    else:
        hb1 = A.bf16(1024)
        hb = [hb1, hb1]
    nhb = 2 if final_out is None else 1
    hT = [A.bf16(8 * TB).rearrange("p (k n) -> p k n", k=8) for s in range(2)]
    jk = A.bf16(1024)
    scr = [dict(junk=jk, jkey="junk", ss=A.f32(1), rs=A.f32(1), key="scr%d" % s) for s in range(2)]
    if final_out is None:
        rr = [A.f32(512) for i in range(2)]
    else:
        rr1 = A.f32(512)
        rr = [rr1, rr1]
        gf = A.f32(1024)
        self.load_bcast(gf, self.dram["final_norm_g"].rearrange("(o n) -> o n", o=1), "gf")
        fscr = [dict(junk=jk, jkey="junk", ss=A.f32(1), rs=A.f32(1), key="fscr%d" % s) for s in range(2)]
    nrr = 2 if final_out is None else 1
    uT = A.bf16(NM * TB).rearrange("p (k n) -> p k n", k=NM)
    nblk = self.ntok // TB

    def prologue(b):
        s = b % 2
        for t in range(4):
            i = t % 2
            row = b * TB + t * 128
            P.add("sp", lambda e, i=i, row=row: e.dma_start(out=xt[i], in_=xn[row:row + 128, :]), w=[("xt", i)], dma=("xt", i))
            self.norm_tile(xt[i], ("xt", i), gt, "gt", hb[i], ("hb", i % nhb), scr[i])
            self.transpose_tile(hb[i], ("hb", i % nhb), hT[s], ("hT", s), t * 128, 0)

    def ffn1(b):
        s = b % 2
        for m in range(NM):
            bk = 1 + m % 2
            for k in range(8):
                P.add("pe", lambda e, m=m, k=k, bk=bk: e.matmul(self.bank(bk), lhsT=w1[:, k, m * 128:(m + 1) * 128],
                                                               rhs=hT[s][:, k, :], start=(k == 0), stop=(k == 7)),
                      r=self.wk("w1", m * 128, 128) + [("hT", s)], w=[("ps", bk)])
            q = m % nrr
            P.add("act", lambda e, bk=bk, q=q: e.activation(out=rr[q], in_=self.bank(bk), func=AF.Relu),
                  r=[("ps", bk)], w=[("rr", q)])
            P.add("dve", lambda e, m=m, q=q: e.tensor_tensor(out=uT[:, m, :], in0=rr[q], in1=rr[q], op=ALU.mult),
                  r=[("rr", q)], w=[("uT", m)])

    def ffn2(b):
        for t in range(4):
            i = t % 2
            row = b * TB + t * 128
            P.add("sp", lambda e, i=i, row=row: e.dma_start(out=xa[i], in_=xn[row:row + 128, :]), w=[("xa", i)], dma=("xa", i))
            for nh in range(2):
                bk = 3 + nh
                for m in range(NM):
                    P.add("pe", lambda e, m=m, t=t, nh=nh, bk=bk: e.matmul(
                        self.bank(bk), lhsT=uT[:, m, t * 128:(t + 1) * 128], rhs=w2[:, m, nh * 512:(nh + 1) * 512],
                        start=(m == 0), stop=(m == NM - 1)), r=[("uT", m)] + self.wk("w2", nh * 512, 512), w=[("ps", bk)])
                P.add("dve", lambda e, i=i, nh=nh, bk=bk: e.tensor_tensor(
                    out=xa[i][:, nh * 512:(nh + 1) * 512], in0=self.bank(bk), in1=xa[i][:, nh * 512:(nh + 1) * 512],
                    op=ALU.add), r=[("ps", bk), ("xa", i)], w=[("xa", i)])
            if final_out is None:
                P.add("sp", lambda e, i=i, row=row: e.dma_start(out=xo[row:row + 128, :], in_=xa[i]),
                      r=[("xa", i)], w=[("dram", id(xo), row)], dma=("xst", i))
            else:
                self.norm_tile(xa[i], ("xa", i), gf, "gf", xa[i], ("xa", i), fscr[i])
                P.add("sp", lambda e, i=i, row=row: e.dma_start(out=final_out[row:row + 128, :], in_=xa[i]),
                      r=[("xa", i)], w=[("dram", "out", row)], dma=("xst", i))

    prologue(0)
    for b in range(nblk):
        ffn1(b)
        if b + 1 < nblk:
            prologue(b + 1)
        ffn2(b)


Builder.ffn_full = _ffn_full
```
